# Optimizing a Trainium2 kernel written in Bass

```python
import math
import jax
import jax.numpy as jnp
from jax import lax
import numpy as np

D_MODEL = 1024
BATCH = 32
SEQ = 2048
DEPTH = 4

GRID_W = 64
CTX_LEN = 256
F32 = jnp.float32
NORM_EPS = 1e-6
NEG_INF = -1e30

A_WIDTH = D_MODEL // 2
A_HEAD_DIM = 64
A_HEADS = A_WIDTH // A_HEAD_DIM
A_DECAY_LORA = 64
A_ICLR_LORA = 64
A_GATE_LORA = 128
A_GN_EPS = 64e-5
A_COLS = 3 * A_WIDTH + A_DECAY_LORA + A_ICLR_LORA + A_GATE_LORA
A_SPLITS = [A_WIDTH, 2 * A_WIDTH, 3 * A_WIDTH, 3 * A_WIDTH + A_DECAY_LORA, 3 * A_WIDTH + A_DECAY_LORA + A_ICLR_LORA]

B_WIDTH = D_MODEL // 2
B_HEAD_DIM = 64
B_HEADS = B_WIDTH // B_HEAD_DIM
NA_ROWS = 8
NA_COLS = 16
NA_QBLOCK = 16
NA_KCOLS = 32
B_COLS = 3 * B_WIDTH

C_WIDTH = D_MODEL // 2
C_HEAD_DIM = 128
C_HEADS = C_WIDTH // C_HEAD_DIM
C_CONV = 5
C_CHUNK = 64
C_COLS = 4 * C_WIDTH + 4 * C_HEADS

D_WIDTH = D_MODEL // 2
D_GROUP = 16
D_GROUPS = D_WIDTH // D_GROUP
D_STATE = 64

EVEN_COLS = A_COLS + B_COLS
ODD_COLS = C_COLS + D_WIDTH
FFN_HIDDEN = -(-8 * D_MODEL // (3 * 256)) * 256

kernel_name = "hybrid_rwkv7_natten_gdn_s5_dit_trunk"


def rmsnorm(x, g):
    x32 = x.astype(F32)
    y = x32 * lax.rsqrt(jnp.mean(x32 * x32, axis=-1, keepdims=True) + NORM_EPS)
    return y.astype(x.dtype) * g


def l2norm(x):
    x32 = x.astype(F32)
    return x32 * lax.rsqrt(jnp.sum(x32 * x32, axis=-1, keepdims=True) + 1e-12)


def modulate(h, shift, scale):
    return h * (1.0 + scale) + shift


def swiglu(h, w_gate, w_up, w_down):
    return (jax.nn.silu(h @ w_gate) * (h @ w_up)) @ w_down


def grid_shift(y, rows):
    b, t, ch = y.shape
    g = y.reshape(b, rows, GRID_W, ch // 4, 4)
    left = jnp.pad(g[:, :, :-1, :, 0], ((0, 0), (0, 0), (1, 0), (0, 0)))
    right = jnp.pad(g[:, :, 1:, :, 1], ((0, 0), (0, 0), (0, 1), (0, 0)))
    up = jnp.pad(g[:, :-1, :, :, 2], ((0, 0), (1, 0), (0, 0), (0, 0)))
    down = jnp.pad(g[:, 1:, :, :, 3], ((0, 0), (0, 1), (0, 0), (0, 0)))
    return jnp.stack([left, right, up, down], axis=-1).reshape(b, t, ch)


def seq_shift(y):
    b, t, ch = y.shape
    g = y.reshape(b, t, ch // 4, 4)
    prev = jnp.pad(g[:, :-1], ((0, 0), (1, 0), (0, 0), (0, 0)))
    nxt = jnp.pad(g[:, 1:], ((0, 0), (0, 1), (0, 0), (0, 0)))
    return jnp.where(np.array([True, False, True, False]), prev, nxt).reshape(b, t, ch)


def rwkv7_scan(s0, r, w, k, v, a, b, reverse):
    def step(s, inp):
        r_t, w_t, k_t, v_t, a_t, b_t = inp
        sa = jnp.einsum('bhvk,bhk->bhv', s, a_t)
        s = s * w_t[:, :, None, :] + sa[..., None] * b_t[:, :, None, :] + v_t[..., None] * k_t[:, :, None, :]
        return s, jnp.einsum('bhvk,bhk->bhv', s, r_t)
    return lax.scan(step, s0, (r, w, k, v, a, b), reverse=reverse)


def rwkv7_mixer(p_l, p_c, rows, mu, w0, w2, a0, a2, g2, k_k, k_a, r_k, ln_g, ln_b, with_ctx):
    p_l = p_l + (grid_shift(p_l, rows) - p_l) * mu
    p_c = p_c + (seq_shift(p_c) - p_c) * mu

    def prepare(p):
        bsz, t, _ = p.shape
        heads = lambda z: z.reshape(bsz, t, A_HEADS, A_HEAD_DIM).astype(F32)
        r, k, v, xw, xa, xg = jnp.split(p, A_SPLITS, axis=-1)
        kk = l2norm(heads(k * k_k))
        tw = jnp.tanh(xw)
        dirs = []
        for d in range(2):
            w_log = -jax.nn.softplus(-(w0[d] + tw @ w2[d])) - 0.5
            a = jax.nn.sigmoid(a0[d] + xa @ a2[d])
            decay = jnp.exp(-jnp.exp(w_log.astype(F32)))
            dirs.append((heads(decay), heads(k * (1.0 + (a - 1.0) * k_a)), heads(a)))
        return heads(r), heads(v), kk, dirs, jax.nn.sigmoid(xg) @ g2

    r_l, v_l, kk_l, dirs_l, g_l = prepare(p_l)
    r_c, v_c, kk_c, dirs_c, g_c = prepare(p_c)
    tm = lambda z: jnp.swapaxes(z, 0, 1)
    s0 = jnp.zeros((p_c.shape[0], A_HEADS, A_HEAD_DIM, A_HEAD_DIM), F32)
    y_l, y_c = 0.0, 0.0
    for d in range(2):
        dec_c, kt_c, a_c = dirs_c[d]
        dec_l, kt_l, a_l = dirs_l[d]
        s_c, yc = rwkv7_scan(s0, tm(r_c), tm(dec_c), tm(kt_c), tm(v_c), tm(-kk_c), tm(kk_c * a_c), d == 1)
        _, yl = rwkv7_scan(s_c, tm(r_l), tm(dec_l), tm(kt_l), tm(v_l), tm(-kk_l), tm(kk_l * a_l), d == 1)
        y_l, y_c = y_l + tm(yl), y_c + tm(yc)

    def readout(y, r, v, dirs, g):
        mean = jnp.mean(y, axis=-1, keepdims=True)
        var = jnp.mean(jnp.square(y - mean), axis=-1, keepdims=True)
        y = (y - mean) * lax.rsqrt(var + A_GN_EPS) * ln_g.reshape(A_HEADS, A_HEAD_DIM) + ln_b.reshape(A_HEADS, A_HEAD_DIM)
        bonus = sum(jnp.sum(r * kt * r_k, axis=-1, keepdims=True) for _, kt, _ in dirs) * v
        return ((y + bonus).reshape(g.shape) * g).astype(g.dtype)

    out_l = readout(y_l, r_l, v_l, dirs_l, g_l)
    out_c = readout(y_c, r_c, v_c, dirs_c, g_c) if with_ctx else None
    return out_l, out_c


def na_mixer(p_l, p_c, rows, rpb, with_ctx):
    bsz, t, _ = p_l.shape
    scale = B_HEAD_DIM ** -0.5
    q_l, k_l, v_l = [z.reshape(bsz, t, B_HEADS, B_HEAD_DIM) for z in jnp.split(p_l, 3, axis=-1)]
    q_c, k_c, v_c = [z.reshape(bsz, -1, B_HEADS, B_HEAD_DIM) for z in jnp.split(p_c, 3, axis=-1)]
    wr = min(NA_ROWS, rows)
    nqb = GRID_W // NA_QBLOCK
    qcol = np.arange(GRID_W).reshape(nqb, NA_QBLOCK)
    kstart = np.clip(np.arange(nqb) * NA_QBLOCK - NA_COLS // 2, 0, GRID_W - NA_KCOLS)
    kcol = kstart[:, None] + np.arange(NA_KCOLS)
    wstart = np.clip(qcol - NA_COLS // 2, 0, GRID_W - NA_COLS)
    col_ok = (kcol[:, None, :] >= wstart[:, :, None]) & (kcol[:, None, :] < wstart[:, :, None] + NA_COLS)
    dc_idx = np.clip(kcol[:, None, :] - qcol[:, :, None] + NA_COLS - 1, 0, 2 * NA_COLS - 2)
    qg = q_l.reshape(bsz, rows, GRID_W, B_HEADS, B_HEAD_DIM)
    kgc = k_l.reshape(bsz, rows, GRID_W, B_HEADS, B_HEAD_DIM)[:, :, kcol]
    vgc = v_l.reshape(bsz, rows, GRID_W, B_HEADS, B_HEAD_DIM)[:, :, kcol]
    n_loc = wr * NA_KCOLS

    def row_block(i):
        rs = jnp.clip(i - wr // 2, 0, rows - wr)
        kb = lax.dynamic_slice_in_dim(kgc, rs, wr, axis=1)
        vb = lax.dynamic_slice_in_dim(vgc, rs, wr, axis=1)
        qb = lax.dynamic_index_in_dim(qg, i, axis=1, keepdims=False).reshape(bsz, nqb, NA_QBLOCK, B_HEADS, B_HEAD_DIM)
        s_loc = jnp.einsum('bmqhd,brmkhd->bhmqrk', qb, kb).astype(F32) * scale
        dr_idx = rs + jnp.arange(wr) - i + NA_ROWS - 1
        bias = jnp.take(jnp.take(rpb, dr_idx, axis=1), dc_idx, axis=2).transpose(0, 2, 3, 1, 4)
        s_loc = jnp.where(col_ok[:, :, None, :], s_loc + bias.astype(F32), NEG_INF)
        s_ctx = jnp.einsum('bmqhd,blhd->bhmql', qb, k_c).astype(F32) * scale
        p = jax.nn.softmax(jnp.concatenate([s_loc.reshape(bsz, B_HEADS, nqb, NA_QBLOCK, n_loc), s_ctx], axis=-1), axis=-1)
        p_loc = p[..., :n_loc].reshape(s_loc.shape).astype(vb.dtype)
        p_ctx = p[..., n_loc:].astype(v_c.dtype)
        o = jnp.einsum('bhmqrk,brmkhd->bmqhd', p_loc, vb) + jnp.einsum('bhmql,blhd->bmqhd', p_ctx, v_c)
        return o.reshape(bsz, GRID_W, B_WIDTH)

    y_l = lax.map(row_block, jnp.arange(rows))
    y_l = jnp.swapaxes(y_l, 0, 1).reshape(bsz, t, B_WIDTH)
    y_c = None
    if with_ctx:
        s = jnp.einsum('bqhd,bkhd->bhqk', q_c, k_c).astype(F32) * scale
        p = jax.nn.softmax(s, axis=-1).astype(v_c.dtype)
        y_c = jnp.einsum('bhqk,bkhd->bqhd', p, v_c).reshape(bsz, -1, B_WIDTH)
    return y_l, y_c


def short_conv(z, w):
    return lax.conv_general_dilated(z, w[:, None, :].astype(z.dtype), (1,), [(C_CONV // 2, C_CONV // 2)],
                                    dimension_numbers=('NWC', 'WIO', 'NWC'), feature_group_count=z.shape[-1])


def gdn_chunked(s0, q, k, v, g, beta):
    bsz, t, h, _ = q.shape
    dv = v.shape[-1]
    n = t // C_CHUNK

    def blocks(z):
        z = z.reshape(bsz, n, C_CHUNK, h, *z.shape[3:])
        return jnp.moveaxis(z, (1, 3), (0, 2))

    qb, kb, vb = blocks(q), blocks(k), blocks(v)
    gc = jnp.cumsum(blocks(g), axis=-1)
    bb = blocks(beta)[..., None]
    tril = np.tril(np.ones((C_CHUNK, C_CHUNK), dtype=bool))
    strict = np.tril(np.ones((C_CHUNK, C_CHUNK), dtype=bool), -1)
    decay = jnp.exp(jnp.where(tril, gc[..., :, None] - gc[..., None, :], -jnp.inf))
    kbeta = kb * bb
    lmat = jnp.where(strict, jnp.einsum('nbhik,nbhjk->nbhij', kbeta, kb) * decay, 0.0)
    rhs = jnp.concatenate([vb * bb, kbeta * jnp.exp(gc)[..., None]], axis=-1)
    sol = lax.linalg.triangular_solve(jnp.eye(C_CHUNK, dtype=F32) + lmat, rhs,
                                      left_side=True, lower=True, unit_diagonal=True)
    u, w = sol[..., :dv], sol[..., dv:]
    attn = jnp.where(tril, jnp.einsum('nbhik,nbhjk->nbhij', qb, kb) * decay, 0.0)

    def step(s, inp):
        q_i, k_i, u_i, w_i, g_i, a_i = inp
        v_new = u_i - jnp.einsum('bhck,bhkv->bhcv', w_i, s)
        o = jnp.einsum('bhck,bhkv->bhcv', q_i * jnp.exp(g_i)[..., None], s) + jnp.einsum('bhij,bhjv->bhiv', a_i, v_new)
        g_last = g_i[..., -1:]
        s = s * jnp.exp(g_last)[..., None] + jnp.einsum('bhck,bhcv->bhkv', k_i * jnp.exp(g_last - g_i)[..., None], v_new)
        return s, o

    s, o = lax.scan(step, s0, (qb, kb, u, w, gc, attn))
    return s, jnp.moveaxis(o, (0, 2), (1, 3)).reshape(bsz, t, h, dv)


def gdn_mixer(p_l, p_c, conv_w, a_log, dt_bias, o_norm, with_ctx):
    def prepare(p):
        bsz, t, _ = p.shape
        heads = lambda z: z.reshape(bsz, t, C_HEADS, C_HEAD_DIM)
        qkv = jax.nn.silu(short_conv(p[..., :3 * C_WIDTH], conv_w)).astype(F32)
        q, k, v = jnp.split(qkv, 3, axis=-1)
        q = l2norm(heads(q)) * C_HEAD_DIM ** -0.5
        k = l2norm(heads(k))
        z = p[..., 3 * C_WIDTH:4 * C_WIDTH]
        gates = p[..., 4 * C_WIDTH:].astype(F32).reshape(bsz, t, 4, C_HEADS)
        dirs = [(-jnp.exp(a_log[d]) * jax.nn.softplus(gates[:, :, 2 + d] + dt_bias[d]),
                 jax.nn.sigmoid(gates[:, :, d])) for d in range(2)]
        return q, k, heads(v), z, dirs

    q_l, k_l, v_l, z_l, dirs_l = prepare(p_l)
    q_c, k_c, v_c, z_c, dirs_c = prepare(p_c)
    s0 = jnp.zeros((p_c.shape[0], C_HEADS, C_HEAD_DIM, C_HEAD_DIM), F32)
    o_l, o_c = 0.0, 0.0
    for d in range(2):
        f = (lambda z: jnp.flip(z, axis=1)) if d == 1 else (lambda z: z)
        g_c, b_c = dirs_c[d]
        g_l, b_l = dirs_l[d]
        s_c, oc = gdn_chunked(s0, f(q_c), f(k_c), f(v_c), f(g_c), f(b_c))
        _, ol = gdn_chunked(s_c, f(q_l), f(k_l), f(v_l), f(g_l), f(b_l))
        o_l, o_c = o_l + f(ol), o_c + f(oc)

    def readout(o, z):
        o = o * lax.rsqrt(jnp.mean(o * o, axis=-1, keepdims=True) + NORM_EPS) * o_norm
        return (o.reshape(z.shape) * jax.nn.silu(z.astype(F32))).astype(z.dtype)

    return readout(o_l, z_l), (readout(o_c, z_c) if with_ctx else None)


def s5_combine(e1, e2):
    a1r, a1i, b1r, b1i = e1
    a2r, a2i, b2r, b2i = e2
    return (a1r * a2r - a1i * a2i, a1r * a2i + a1i * a2r,
            a2r * b1r - a2i * b1i + b2r, a2r * b1i + a2i * b1r + b2i)


def s5_scan(abar, bbar, u, x0, reverse):
    abr, abi = abar
    br = jnp.einsum('btgi,gpi->btgp', u, bbar[0])
    bi = jnp.einsum('btgi,gpi->btgp', u, bbar[1])
    t = u.shape[1]
    if x0 is not None:
        idx = t - 1 if reverse else 0
        br = br.at[:, idx].add(abr * x0[0] - abi * x0[1])
        bi = bi.at[:, idx].add(abr * x0[1] + abi * x0[0])
    ar = jnp.broadcast_to(abr, (1, t) + abr.shape)
    ai = jnp.broadcast_to(abi, (1, t) + abi.shape)
    _, _, xr, xi = lax.associative_scan(s5_combine, (ar, ai, br, bi), reverse=reverse, axis=1)
    return xr, xi


def s5_mixer(u_l, u_c, a_re, a_im, log_dt, b_re, b_im, c_re, c_im, d_skip, w_glu, with_ctx):
    grp = lambda u: u.astype(F32).reshape(u.shape[0], u.shape[1], D_GROUPS, D_GROUP)
    ul, uc = grp(u_l), grp(u_c)
    xl_r = xl_i = xc_r = xc_i = 0.0
    for d in range(2):
        dt = jnp.exp(log_dt[d])[:, None]
        ar, ai = a_re[d], a_im[d]
        mag = jnp.exp(ar * dt)
        abr, abi = mag * jnp.cos(ai * dt), mag * jnp.sin(ai * dt)
        den = ar * ar + ai * ai
        cr = ((abr - 1.0) * ar + abi * ai) / den
        ci = (abi * ar - (abr - 1.0) * ai) / den
        bbar = (cr[..., None] * b_re - ci[..., None] * b_im, cr[..., None] * b_im + ci[..., None] * b_re)
        rev = d == 1
        cxr, cxi = s5_scan((abr, abi), bbar, uc, None, rev)
        end = 0 if rev else -1
        lxr, lxi = s5_scan((abr, abi), bbar, ul, (cxr[:, end], cxi[:, end]), rev)
        xl_r, xl_i, xc_r, xc_i = xl_r + lxr, xl_i + lxi, xc_r + cxr, xc_i + cxi

    def readout(xr, xi, u):
        y = (jnp.einsum('btgp,gip->btgi', xr, c_re) - jnp.einsum('btgp,gip->btgi', xi, c_im)
             + d_skip.reshape(D_GROUPS, D_GROUP) * u)
        y = jax.nn.gelu(y.reshape(y.shape[0], y.shape[1], D_WIDTH))
        return y * jax.nn.sigmoid(y @ w_glu)

    out_l = readout(xl_r, xl_i, ul).astype(u_l.dtype)
    out_c = readout(xc_r, xc_i, uc).astype(u_c.dtype) if with_ctx else None
    return out_l, out_c


def setup_inputs(seed: int = 0) -> dict:
    key = jax.random.key(seed)
    keys = iter(jax.random.split(key, 48))

    def nrm(shape, std):
        return jax.random.normal(next(keys), shape, jnp.float32) * std

    def uni(shape, lo, hi):
        return jax.random.uniform(next(keys), shape, jnp.float32, lo, hi)

    ne, no, d = (DEPTH + 1) // 2, DEPTH // 2, D_MODEL
    dt_gdn = jnp.exp(uni((no, 2, C_HEADS), math.log(1e-3), math.log(1e-1)))
    return {
        "x": nrm((BATCH, SEQ, d), 1.0),
        "c": nrm((BATCH, d), 1.0),
        "ctx": nrm((BATCH, CTX_LEN, d), 1.0),
        "c_ctx": nrm((d,), 1.0),
        "mod_w": nrm((DEPTH, d, 6 * d), 0.5 * d ** -0.5),
        "mod_b": nrm((DEPTH, 6 * d), 0.02),
        "norm_mix": 1.0 + nrm((DEPTH, d), 0.02),
        "norm_ffn": 1.0 + nrm((DEPTH, d), 0.02),
        "w_in_even": nrm((ne, d, EVEN_COLS), d ** -0.5),
        "w_in_odd": nrm((no, d, ODD_COLS), d ** -0.5),
        "w_out": nrm((DEPTH, d, d), d ** -0.5),
        "a_mu": uni((ne, A_COLS), 0.1, 0.9),
        "a_w0": uni((ne, 2, A_WIDTH), -6.0, -1.0),
        "a_w2": nrm((ne, 2, A_DECAY_LORA, A_WIDTH), 0.1 * A_DECAY_LORA ** -0.5),
        "a_a0": nrm((ne, 2, A_WIDTH), 0.1),
        "a_a2": nrm((ne, 2, A_ICLR_LORA, A_WIDTH), A_ICLR_LORA ** -0.5),
        "a_g2": nrm((ne, A_GATE_LORA, A_WIDTH), A_GATE_LORA ** -0.5),
        "a_kk": 0.85 + nrm((ne, A_WIDTH), 0.02),
        "a_ka": 1.0 + nrm((ne, A_WIDTH), 0.02),
        "a_rk": nrm((ne, A_HEADS, A_HEAD_DIM), 0.1),
        "a_lnx_g": 1.0 + nrm((ne, A_WIDTH), 0.02),
        "a_lnx_b": nrm((ne, A_WIDTH), 0.02),
        "b_rpb": nrm((ne, B_HEADS, 2 * NA_ROWS - 1, 2 * NA_COLS - 1), 0.02),
        "c_conv": nrm((no, C_CONV, 3 * C_WIDTH), C_CONV ** -0.5),
        "c_alog": jnp.log(uni((no, 2, C_HEADS), 1.0, 16.0)),
        "c_dtb": dt_gdn + jnp.log(-jnp.expm1(-dt_gdn)),
        "c_onorm": 1.0 + nrm((no, C_HEAD_DIM), 0.02),
        "d_are": -0.5 + nrm((no, 2, D_GROUPS, D_STATE), 0.01),
        "d_aim": jnp.pi * jnp.arange(D_STATE, dtype=jnp.float32) + nrm((no, 2, D_GROUPS, D_STATE), 0.01),
        "d_logdt": uni((no, 2, D_GROUPS), math.log(1e-3), math.log(1e-1)),
        "d_bre": nrm((no, D_GROUPS, D_STATE, D_GROUP), (2 * D_GROUP) ** -0.5),
        "d_bim": nrm((no, D_GROUPS, D_STATE, D_GROUP), (2 * D_GROUP) ** -0.5),
        "d_cre": nrm((no, D_GROUPS, D_GROUP, D_STATE), D_STATE ** -0.5),
        "d_cim": nrm((no, D_GROUPS, D_GROUP, D_STATE), D_STATE ** -0.5),
        "d_d": nrm((no, D_WIDTH), 1.0),
        "d_glu": nrm((no, D_WIDTH, D_WIDTH), D_WIDTH ** -0.5),
        "f_gate": nrm((DEPTH, d, FFN_HIDDEN), d ** -0.5),
        "f_up": nrm((DEPTH, d, FFN_HIDDEN), d ** -0.5),
        "f_down": nrm((DEPTH, FFN_HIDDEN, d), FFN_HIDDEN ** -0.5),
        "final_norm": 1.0 + nrm((d,), 0.02),
    }


def reference(x, c, ctx, c_ctx, mod_w, mod_b, norm_mix, norm_ffn, w_in_even, w_in_odd, w_out,
              a_mu, a_w0, a_w2, a_a0, a_a2, a_g2, a_kk, a_ka, a_rk, a_lnx_g, a_lnx_b, b_rpb,
              c_conv, c_alog, c_dtb, c_onorm, d_are, d_aim, d_logdt, d_bre, d_bim, d_cre, d_cim, d_d, d_glu,
              f_gate, f_up, f_down, final_norm):
    rows = x.shape[1] // GRID_W
    h_lat, h_ctx = x, ctx
    cond_lat = jax.nn.silu(c)[:, None, :]
    cond_ctx = jax.nn.silu(c_ctx)[None, None, :]
    for l in range(DEPTH):
        ctx_out = l < DEPTH - 1
        m_l = jnp.split(cond_lat @ mod_w[l] + mod_b[l], 6, axis=-1)
        m_c = jnp.split(cond_ctx @ mod_w[l] + mod_b[l], 6, axis=-1)
        n_l = modulate(rmsnorm(h_lat, norm_mix[l]), m_l[0], m_l[1])
        n_c = modulate(rmsnorm(h_ctx, norm_mix[l]), m_c[0], m_c[1])
        if l % 2 == 0:
            e = l // 2
            p_l, p_c = n_l @ w_in_even[e], n_c @ w_in_even[e]
            y1_l, y1_c = rwkv7_mixer(p_l[..., :A_COLS], p_c[..., :A_COLS], rows, a_mu[e], a_w0[e], a_w2[e],
                                     a_a0[e], a_a2[e], a_g2[e], a_kk[e], a_ka[e], a_rk[e], a_lnx_g[e], a_lnx_b[e], ctx_out)
            y2_l, y2_c = na_mixer(p_l[..., A_COLS:], p_c[..., A_COLS:], rows, b_rpb[e], ctx_out)
        else:
            o = l // 2
            p_l, p_c = n_l @ w_in_odd[o], n_c @ w_in_odd[o]
            y1_l, y1_c = gdn_mixer(p_l[..., :C_COLS], p_c[..., :C_COLS], c_conv[o], c_alog[o], c_dtb[o], c_onorm[o], ctx_out)
            y2_l, y2_c = s5_mixer(p_l[..., C_COLS:], p_c[..., C_COLS:], d_are[o], d_aim[o], d_logdt[o], d_bre[o], d_bim[o],
                                  d_cre[o], d_cim[o], d_d[o], d_glu[o], ctx_out)
        h_lat = h_lat + m_l[2] * (jnp.concatenate([y1_l, y2_l], axis=-1) @ w_out[l])
        h_lat = h_lat + m_l[5] * swiglu(modulate(rmsnorm(h_lat, norm_ffn[l]), m_l[3], m_l[4]), f_gate[l], f_up[l], f_down[l])
        if ctx_out:
            h_ctx = h_ctx + m_c[2] * (jnp.concatenate([y1_c, y2_c], axis=-1) @ w_out[l])
            h_ctx = h_ctx + m_c[5] * swiglu(modulate(rmsnorm(h_ctx, norm_ffn[l]), m_c[3], m_c[4]), f_gate[l], f_up[l], f_down[l])
    return rmsnorm(h_lat, final_norm)
```

```python
import numpy as np
from contextlib import ExitStack
import concourse.bass as bass
import concourse.mybir as mybir
from concourse.bass_utils import run_bass_kernel_spmd

F32 = mybir.dt.float32
BF16 = mybir.dt.bfloat16
I32 = mybir.dt.int32
AF = mybir.ActivationFunctionType
ALU = mybir.AluOpType
AX = mybir.AxisListType

SAME_ENG_SYNC = True
N_DMA_SLOTS = 8
DBG = {"stage": 99}


class Reg:
    __slots__ = ("w", "r")

    def __init__(self):
        self.w = None
        self.r = {}


class View:
    __slots__ = ("ap", "regs")

    def __init__(self, ap, regs):
        self.ap = ap
        self.regs = regs


class Tile:
    def __init__(self, P, name, shape, dtype, space="sbuf", nreg=1):
        self.P = P
        self.name = name
        self.shape = shape
        self.dtype = dtype
        if space == "sbuf":
            self.t = P.es.enter_context(P.nc.sbuf_tensor(name, list(shape), dtype))
        else:
            self.t = P.es.enter_context(P.nc.psum_tensor(name, list(shape), dtype))
        self.regs = [Reg() for _ in range(nreg)]

    def v(self, r=0, idx=None):
        if isinstance(r, int):
            regs = [self.regs[r]]
        else:
            regs = [self.regs[i] for i in r]
        ap = self.t[:] if idx is None else self.t[idx]
        return View(ap, regs)

    def all(self, idx=None):
        ap = self.t[:] if idx is None else self.t[idx]
        return View(ap, self.regs)


def _ap(x):
    return x.ap if isinstance(x, View) else x


class Prog:
    ENGS = ("pe", "act", "dve", "pool", "sp")

    def __init__(self, nc):
        self.nc = nc
        self.es = ExitStack()
        self.ops = {e: [] for e in self.ENGS}
        self.ncomp = {e: 0 for e in self.ENGS}
        self.ndma = {e: 0 for e in self.ENGS}
        self.waited = {e: {} for e in self.ENGS}
        self.semkeys = set()
        self.dma_last = {}

    def tile(self, name, shape, dtype, nreg=1):
        return Tile(self, name, shape, dtype, "sbuf", nreg)

    def psum(self, name, shape, dtype, nreg=1):
        return Tile(self, name, shape, dtype, "psum", nreg)

    def op(self, eng, fn, reads, writes, dma=False):
        deps = {}

        def add(tok):
            if tok is None:
                return
            sk, val, deng, ddma = tok
            if (not ddma) and deng == eng and not dma:
                if eng == "pe" or not SAME_ENG_SYNC:
                    return
            if deps.get(sk, 0) < val:
                deps[sk] = val

        for v in reads:
            if isinstance(v, View):
                for reg in v.regs:
                    add(reg.w)
        for v in writes:
            if isinstance(v, View):
                for reg in v.regs:
                    add(reg.w)
                    for tok in reg.r.values():
                        add(tok)
        if dma:
            k = self.ndma[eng]
            self.ndma[eng] += 1
            slot = k % N_DMA_SLOTS
            sk = ("dma", eng, slot)
            val = 16 * (k // N_DMA_SLOTS + 1)
            if k >= N_DMA_SLOTS:
                if deps.get(sk, 0) < val - 16:
                    deps[sk] = val - 16
            inc = (sk, 16)
            self.dma_last[sk] = val
        else:
            self.ncomp[eng] += 1
            sk = ("eng", eng)
            val = self.ncomp[eng]
            inc = (sk, 1)
        self.semkeys.add(sk)
        tok = (sk, val, eng, dma)
        waits = []
        wd = self.waited[eng]
        for dsk, dval in deps.items():
            if wd.get(dsk, 0) < dval:
                wd[dsk] = dval
                waits.append((dsk, dval))
        for v in reads:
            if isinstance(v, View):
                for reg in v.regs:
                    old = reg.r.get(sk)
                    if old is None or old[1] < val:
                        reg.r[sk] = tok
        for v in writes:
            if isinstance(v, View):
                for reg in v.regs:
                    reg.w = tok
                    reg.r = {}
        self.ops[eng].append((waits, fn, inc))
        return tok

    def dma(self, out, in_, queue="sp", final=False):
        o, i = _ap(out), _ap(in_)
        return self.op(queue, lambda e: e.dma_start(out=o, in_=i), [in_], [out], dma=True)

    def mm(self, out, lhsT, rhs, start=True, stop=True):
        o, l, r = _ap(out), _ap(lhsT), _ap(rhs)
        return self.op("pe", lambda e: e.matmul(o, l, r, start=start, stop=stop), [lhsT, rhs], [out])

    def transpose(self, out, in_, ident):
        o, i, d = _ap(out), _ap(in_), _ap(ident)
        return self.op("pe", lambda e: e.transpose(o, i, d), [in_, ident], [out])

    def act(self, out, in_, func, bias=None, scale=None, accum_out=None):
        o, i = _ap(out), _ap(in_)
        kw = {}
        rd = [in_]
        wr = [out]
        if bias is not None:
            kw["bias"] = _ap(bias)
            rd.append(bias)
        if scale is not None:
            kw["scale"] = _ap(scale)
            rd.append(scale)
        if accum_out is not None:
            kw["accum_out"] = _ap(accum_out)
            wr.append(accum_out)
        return self.op("act", lambda e: e.activation(out=o, in_=i, func=func, **kw), rd, wr)

    def tt(self, eng, out, in0, in1, op):
        o, a, b = _ap(out), _ap(in0), _ap(in1)
        return self.op(eng, lambda e: e.tensor_tensor(out=o, in0=a, in1=b, op=op), [in0, in1], [out])

    def ts(self, eng, out, in0, s1, op0, s2=None, op1=None, accum_out=None):
        o, a = _ap(out), _ap(in0)
        kw = {}
        wr = [out]
        if accum_out is not None:
            kw["accum_out"] = _ap(accum_out)
            wr.append(accum_out)
        if op1 is None:
            fn = lambda e: e.tensor_scalar(out=o, in0=a, scalar1=_ap(s1), scalar2=None, op0=op0, **kw)
        else:
            fn = lambda e: e.tensor_scalar(out=o, in0=a, scalar1=_ap(s1), scalar2=_ap(s2), op0=op0, op1=op1, **kw)
        return self.op(eng, fn, [in0, s1, s2], wr)

    def stt(self, out, in0, scalar, in1, op0, op1, eng="dve"):
        o, a, b = _ap(out), _ap(in0), _ap(in1)
        return self.op(eng, lambda e: e.scalar_tensor_tensor(out=o, in0=a, scalar=_ap(scalar), in1=b, op0=op0, op1=op1),
                       [in0, scalar, in1], [out])

    def copy(self, eng, out, in_):
        o, i = _ap(out), _ap(in_)
        if eng == "act":
            return self.op(eng, lambda e: e.copy(out=o, in_=i), [in_], [out])
        return self.op(eng, lambda e: e.tensor_copy(out=o, in_=i), [in_], [out])

    def memset(self, eng, out, val):
        o = _ap(out)
        return self.op(eng, lambda e: e.memset(o, val), [], [out])

    def reduce(self, out, in_, op, axis=AX.X, eng="dve", negate=None):
        o, i = _ap(out), _ap(in_)
        return self.op(eng, lambda e: e.tensor_reduce(out=o, in_=i, axis=axis, op=op, negate=negate), [in_], [out])

    def scan(self, out, d0, d1, initial, op0, op1):
        o, a, b = _ap(out), _ap(d0), _ap(d1)
        return self.op("dve", lambda e: e.tensor_tensor_scan(out=o, data0=a, data1=b, initial=_ap(initial), op0=op0, op1=op1),
                       [d0, d1, initial], [out])

    def range_wrap(self, out, in_, shift, bound, period):
        o, i = _ap(out), _ap(in_)
        return self.op("dve", lambda e: e.add_range_wrap(o, i, shift, bound, period), [in_], [out])

    def cody_waite(self, out, x, k, c1, c2, c3):
        o, a, b = _ap(out), _ap(x), _ap(k)
        return self.op("dve", lambda e: e.cody_waite_cascade(o, a, b, c1, c2, c3), [x, k], [out])

    def recip(self, out, in_):
        o, i = _ap(out), _ap(in_)
        return self.op("dve", lambda e: e.reciprocal(out=o, in_=i), [in_], [out])

    def emit(self):
        nc = self.nc
        sems = {}
        for sk in sorted(self.semkeys, key=str):
            nm = "s_" + "_".join(str(x) for x in sk)
            sems[sk] = self.es.enter_context(nc.semaphore(nm))
        block = self.es.enter_context(nc.Block())
        ops = self.ops

        def body(engname):
            def f(e):
                for waits, fn, inc in ops[engname]:
                    for sk, val in waits:
                        e.wait_ge(sems[sk], val)
                    ins = fn(e)
                    ins.then_inc(sems[inc[0]], inc[1])
                if engname == "sp":
                    for sk, val in self.dma_last.items():
                        e.wait_ge(sems[sk], val)
            return f

        block.sync(body("sp"))
        block.tensor(body("pe"))
        block.scalar(body("act"))
        block.vector(body("dve"))
        block.gpsimd(body("pool"))
        self.es.close()


class Arena:
    G = 64

    def __init__(self, P, nwords):
        self.P = P
        self.nwords = nwords
        self.t = P.es.enter_context(P.nc.sbuf_tensor("arena", [128, nwords], F32))
        self.regs = [Reg() for _ in range((nwords + self.G - 1) // self.G)]
        self.top = 0

    def alloc(self, shape, dtype, off=None):
        if isinstance(shape, int):
            shape = (shape,)
        esz = 2 if dtype == BF16 else 4
        n = int(np.prod(shape))
        nwords = (n * esz + 3) // 4
        nwords = (nwords + self.G - 1) // self.G * self.G
        if off is None:
            off = self.top
            self.top += nwords
            assert self.top <= self.nwords, ("arena overflow", self.top, self.nwords)
        return AVar(self, off, tuple(shape), dtype, nwords)


class AVar:
    def __init__(self, arena, off, shape, dtype, nwords):
        self.arena = arena
        self.off = off
        self.shape = shape
        self.dtype = dtype
        self.esz = 2 if dtype == BF16 else 4
        self.nwords = nwords
        n = int(np.prod(shape))
        nw = (n * self.esz + 3) // 4
        base = arena.t[:, off:off + nw]
        if dtype != F32:
            base = base.bitcast(dtype)
        if len(shape) == 2:
            base = base.rearrange("p (a b) -> p a b", b=shape[1])
        elif len(shape) == 3:
            base = base.rearrange("p (a b c) -> p a b c", b=shape[1], c=shape[2])
        self.ap = base
        st = []
        s = 1
        for d in reversed(shape):
            st.append(s)
            s *= d
        self.strides = tuple(reversed(st))

    def __getitem__(self, idx):
        if not isinstance(idx, tuple):
            idx = (idx,)
        ap = self.ap[idx]
        lo = hi = 0
        for i, d in enumerate(self.shape):
            ix = idx[i + 1] if i + 1 < len(idx) else slice(None)
            if isinstance(ix, int):
                a = b = ix
            else:
                a, b, stp = ix.indices(d)
                b = b - 1
            lo += a * self.strides[i]
            hi += b * self.strides[i]
        wl = self.off + (lo * self.esz) // 4
        wh = self.off + (hi * self.esz + self.esz - 1) // 4
        G = self.arena.G
        return View(ap, self.arena.regs[wl // G: wh // G + 1])

    def full(self):
        return self[(slice(None),)]

    def rev2(self, ps_, row):
        v = self[ps_, row, :]
        return View(self.ap[ps_, row, ::-1], v.regs)

    def bcast_last(self, ps_, row, nch, ch, pos):
        v = self[ps_, row, :]
        a = self.ap[ps_, row, :].rearrange("p (c i) -> p c i", i=ch)[:, :, pos:pos + 1].to_broadcast([v.ap.shape[0], nch, ch])
        return View(a, v.regs)

    def rev3(self, row, c0, N):
        v = self[:, row, c0:c0 + N]
        stop = c0 - 1 if c0 > 0 else None
        return View(self.ap[:, row, c0 + N - 1:stop:-1], v.regs)

    def rev(self, ps_, c0, N):
        v = self[ps_, c0:c0 + N]
        stop = c0 - 1 if c0 > 0 else None
        return View(self.ap[ps_, c0 + N - 1:stop:-1], v.regs)


D = 1024
KC = 8
SEQ = 2048
CTX = 256
T = SEQ + CTX
DEPTH = 4
FFN = 2816
FJ = FFN // 128
EVEN_COLS = 3328
ODD_COLS = 2576
NORM_EPS = 1e-6

TT = [(0, 512), (512, 512), (1024, 512), (1536, 512), (2048, 256)]
FT = 384


def build_program(NB, layers=(0, 1, 2, 3), mixers=True, debug=False):
    nc = bass.Bass("TRN2", target_bir_lowering=False)

    def din(name, shape, dt=F32):
        return nc.dram_tensor(name, list(shape), dt, kind="ExternalInput").ap()

    xT = din("xT", [NB, 128, KC, SEQ])
    ctxT = din("ctxT", [NB, 128, KC, CTX])
    cT = din("cT", [128, KC, NB + 1])
    mod_w = din("mod_w", [DEPTH, D, 6 * D])
    mod_bT = din("mod_bT", [128, DEPTH, 48])
    nmixT = din("nmixT", [128, DEPTH, KC])
    nffnT = din("nffnT", [128, DEPTH, KC])
    fnormT = din("fnormT", [128, KC])
    f_gate = din("f_gate", [DEPTH, D, FFN])
    f_up = din("f_up", [DEPTH, D, FFN])
    f_down = din("f_down", [DEPTH, FFN, D])
    w_in_even = din("w_in_even", [2, D, EVEN_COLS])
    w_in_odd = din("w_in_odd", [2, D, ODD_COLS])
    w_out = din("w_out", [DEPTH, D, D])
    rpbT = din("rpbT", [2, 64, 8, 960])
    maskB_d = din("maskB", [64, 960])
    ident_d = din("ident", [128, 128])
    s5_are2 = din("s5_are2", [2, 128, 2, 32])
    s5_aim2 = din("s5_aim2", [2, 128, 2, 32])
    s5_ldt2 = din("s5_ldt2", [2, 128, 2, 32])
    s5_sgn = din("s5_sgn", [128, 1])
    s5_are_r = din("s5_are_r", [2, 128, 4, 2, 64])
    s5_aim_r = din("s5_aim_r", [2, 128, 4, 2, 64])
    s5_ldt_r = din("s5_ldt_r", [2, 4, 2, 128, 1])
    s5_bre_r = din("s5_bre_r", [2, 128, 4, 64])
    s5_bim_r = din("s5_bim_r", [2, 128, 4, 64])
    s5_c_r = din("s5_c_r", [2, 128, 4, 128])
    s5_dskip = din("s5_dskip", [2, 128, 4])
    s5_rowmask = din("s5_rowmask", [128, 8])
    s5_iota = din("s5_iota", [128, 256])
    d_glu = din("d_glu", [2, 512, 512])
    r_mu = din("r_mu", [2, 128, 14])
    r_slot = din("r_slot", [128, 4])
    r_w0 = din("r_w0", [2, 128, 2, 4])
    r_a0 = din("r_a0", [2, 128, 2, 4])
    r_kk = din("r_kk", [2, 128, 4])
    r_ka = din("r_ka", [2, 128, 4])
    r_rk = din("r_rk", [2, 128, 4])
    r_lng = din("r_lng", [2, 64, 512])
    r_lnb = din("r_lnb", [2, 64, 512])
    a_w2 = din("a_w2", [2, 2, 64, 512])
    a_a2 = din("a_a2", [2, 2, 64, 512])
    a_g2 = din("a_g2", [2, 128, 512])
    r_headsel = din("r_headsel", [128, 2])
    r_MK = din("r_MK", [2, 128, 128])
    r_MN = din("r_MN", [2, 64, 64])
    g_maskLT = din("g_maskLT", [64, 64])
    g_maskUT = din("g_maskUT", [64, 64])
    g_onorm = din("g_onorm", [2, 64, 128])
    g_conv = din("g_conv", [2, 128, 12, 5])
    g_alog = din("g_alog", [2, 16, 1])
    g_dtb = din("g_dtb", [2, 16, 1])
    dbg = nc.dram_tensor("dbg", [128, 4, T], F32, kind="ExternalOutput").ap() if debug else None
    outT = nc.dram_tensor("outT", [NB, 128, KC, SEQ], F32, kind="ExternalOutput").ap()

    P = Prog(nc)
    A = Arena(P, 53184)
    NCOL = NB + 1
    h = A.alloc((KC, T), F32)
    n = A.alloc((KC, T), BF16)
    modT = A.alloc((DEPTH, 48, NCOL), F32)
    mod_b = A.alloc((DEPTH, 48), F32)
    nmix = A.alloc((DEPTH, KC), F32)
    nffn = A.alloc((DEPTH, KC), F32)
    fnorm = A.alloc((KC,), F32)
    ones_bf = A.alloc((128,), BF16)
    sc = A.alloc((KC, NCOL), F32)
    avec = A.alloc((KC,), F32)
    bvec = A.alloc((KC,), F32)
    gvec = A.alloc((KC,), F32)
    sq = A.alloc((KC, 512), BF16)
    rstd = A.alloc((512,), F32)
    tmp = A.alloc((2, 512), F32)
    identf = A.alloc((128,), F32)
    identb = A.alloc((128,), BF16)
    epsb = A.alloc((1,), F32)
    eps12 = A.alloc((1,), F32)
    onescol = A.alloc((1,), F32)
    epsgn = A.alloc((1,), F32)
    phase0 = A.top
    print('phase0 words', phase0)

    banks = [P.psum("ps%d" % i, [128, 512], F32) for i in range(8)]
    bank_i = [0]

    ps_nb = [8]

    def ps():
        b = banks[bank_i[0] % ps_nb[0]]
        bank_i[0] += 1
        return b

    bank3_i = [0]

    def ps3():
        b = banks[bank3_i[0] % 3]
        bank3_i[0] += 1
        return b

    ALL = slice(None)

    P.memset("dve", ones_bf[:, :], 1.0)
    P.dma(identf[:, :], ident_d)
    P.copy("dve", identb[:, :], identf[:, :])
    P.dma(sc[:, :, :], cT)
    P.dma(mod_b[:, :, :], mod_bT)
    P.dma(nmix[:, :, :], nmixT)
    P.dma(nffn[:, :, :], nffnT)
    P.dma(fnorm[:, :], fnormT)
    P.act(sc[:, :, :], sc[:, :, :], AF.Silu)

    MG = 384
    wmod = [A.alloc((KC, MG), F32) for _ in range(2)]
    cnt = 0
    for l in layers:
        wl = mod_w[l].rearrange("(k p) m -> p k m", p=128)
        for mg in range(6 * D // MG):
            wt = wmod[cnt % 2]
            cnt += 1
            P.dma(wt[:, :, :], wl[:, :, mg * MG:(mg + 1) * MG])
            for j in range(MG // 128):
                pb = ps()
                for k in range(KC):
                    P.mm(pb.v(0, (ALL, slice(0, NCOL))), wt[:, k, j * 128:(j + 1) * 128], sc[:, k, :],
                         start=(k == 0), stop=(k == KC - 1))
                jj = mg * (MG // 128) + j
                P.act(modT[:, l, jj, :], pb.v(0, (ALL, slice(0, NCOL))), AF.Identity, bias=mod_b[:, l, jj:jj + 1])
    A.top = phase0

    hid = A.alloc((FJ, 2 * FT), BF16)
    wgu = [A.alloc((2, KC, 128), BF16) for _ in range(2)]
    wdn = [A.alloc((FJ, 128), BF16) for _ in range(2)]
    sg = [A.alloc((FT,), F32) for _ in range(2)]
    wcnt = [0, 0]
    outs = A.alloc((KC, 512), F32)

    ffn_top = A.top
    print("ffn top", ffn_top)
    A.top = phase0
    wo = A.alloc((D,), BF16)
    yT = A.alloc((T,), BF16)
    mixbase = A.top

    def wout_accumulate(l, b, row0, nrows, tend):
        P.dma(wo[0:nrows, :], w_out[l][row0:row0 + nrows, :], queue="pool")
        for (c0, N) in TT:
            if c0 >= tend:
                continue
            col = b if c0 < SEQ else NCOL - 1
            for m in range(KC):
                po_ = ps()
                P.mm(po_.v(0, (ALL, slice(0, N))), wo[0:nrows, m * 128:(m + 1) * 128], yT[0:nrows, c0:c0 + N])
                P.stt(h[:, m, c0:c0 + N], po_.v(0, (ALL, slice(0, N))), modT[:, l, 2 * KC + m, col:col + 1],
                      h[:, m, c0:c0 + N], ALU.mult, ALU.add)

    A.top = mixbase
    na_w = [A.alloc((KC, 128), BF16) for _ in range(3)]
    qT = A.alloc((T,), BF16)
    kT = A.alloc((T,), BF16)
    Vtok = A.alloc((18, 128), BF16)
    Vodd = A.alloc((15, 128), BF16)
    TB = A.alloc((2, 960), F32)
    maskB = A.alloc((960,), F32)
    S_sb = [A.alloc((768,), F32) for _ in range(2)]
    Pe = [A.alloc((768,), F32) for _ in range(2)]
    Pn = [A.alloc((768,), BF16) for _ in range(2)]
    PT = [A.alloc((6, 128), BF16) for _ in range(2)]
    sm = [A.alloc((4,), F32) for _ in range(2)]
    na_top = A.top
    print("na top", na_top)
    na_cnt = [0]

    def bank_bf16(bk, a, b_):
        ap = bk.t[:, 0:(a * b_) // 2].bitcast(BF16).rearrange("p (a b) -> p a b", b=b_)
        return View(ap, bk.regs)

    def softmax_rows(np_, S, W, idx):
        s_, pe_, pn_, sm_ = S_sb[idx], Pe[idx], Pn[idx], sm[idx]
        P.reduce(sm_[0:np_, 0:1], s_[0:np_, 0:W], ALU.max, negate=True)
        P.act(pe_[0:np_, 0:W], s_[0:np_, 0:W], AF.Exp, bias=sm_[0:np_, 0:1], accum_out=sm_[0:np_, 1:2])
        P.recip(sm_[0:np_, 2:3], sm_[0:np_, 1:2])
        P.ts("dve", pn_[0:np_, 0:W], pe_[0:np_, 0:W], sm_[0:np_, 2:3], ALU.mult)
        return pn_

    def na_mixer(l, b):
        e = l // 2
        w_in = w_in_even[e].rearrange("(k p) m -> p k m", p=128)
        with_ctx = l < DEPTH - 1
        P.dma(maskB[0:64, :], maskB_d)
        for hp in range(4):
            for wi, c_off in enumerate((1792, 2304, 2816)):
                P.dma(na_w[wi][:, :, :], w_in[:, :, c_off + hp * 128:c_off + (hp + 1) * 128], queue="pool")
            P.dma(TB[0:64, :, :], rpbT[e][:, 2 * hp:2 * hp + 2, :])
            for hh in range(2):
                P.tt("dve", TB[0:64, hh, :], TB[0:64, hh, :], maskB[0:64, :], ALU.add)
            for (c0, N) in TT:
                pq = ps()
                for k in range(KC):
                    P.mm(pq.v(0, (ALL, slice(0, N))), na_w[0][:, k, :], n[:, k, c0:c0 + N], start=(k == 0), stop=(k == KC - 1))
                P.act(qT[:, c0:c0 + N], pq.v(0, (ALL, slice(0, N))), AF.Copy, scale=0.125)
                pk = ps()
                for k in range(KC):
                    P.mm(pk.v(0, (ALL, slice(0, N))), na_w[1][:, k, :], n[:, k, c0:c0 + N], start=(k == 0), stop=(k == KC - 1))
                P.copy("dve", kT[:, c0:c0 + N], pk.v(0, (ALL, slice(0, N))))
            for tb in range(18 + 15):
                t0 = tb * 128 if tb < 18 else 64 + (tb - 18) * 128
                dst = Vtok[:, tb, :] if tb < 18 else Vodd[:, tb - 18, :]
                pv = ps()
                for k in range(KC):
                    P.mm(pv.v(0, (ALL, slice(0, 128))), n[:, k, t0:t0 + 128], na_w[2][:, k, :], start=(k == 0), stop=(k == KC - 1))
                if tb % 2 == 0:
                    P.copy("act", dst, pv.v(0, (ALL, slice(0, 128))))
                else:
                    P.copy("dve", dst, pv.v(0, (ALL, slice(0, 128))))
            for hh in range(2):
                po = hh * 64
                for i in range(32):
                    idx = na_cnt[0] % 2
                    na_cnt[0] += 1
                    rs = min(max(i - 4, 0), 24)
                    dr0 = rs - i + 7
                    pl = ps()
                    pc = ps()
                    P.mm(pl.v(0, (slice(0, 64), slice(0, 512))), qT[po:po + 64, i * 64:(i + 1) * 64], kT[po:po + 64, rs * 64:rs * 64 + 512])
                    P.mm(pc.v(0, (slice(0, 64), slice(0, 256))), qT[po:po + 64, i * 64:(i + 1) * 64], kT[po:po + 64, SEQ:T])
                    s_ = S_sb[idx]
                    P.tt("dve", s_[0:64, 0:512], pl.v(0, (slice(0, 64), slice(0, 512))), TB[0:64, hh, dr0 * 64:(dr0 + 8) * 64], ALU.add)
                    P.copy("act", s_[0:64, 512:768], pc.v(0, (slice(0, 64), slice(0, 256))))
                    pn_ = softmax_rows(64, s_, 768, idx)
                    ptb = ps()
                    ptv = bank_bf16(ptb, 6, 64)
                    for c in range(6):
                        P.transpose(View(ptv.ap[:, c, :], ptv.regs), pn_[0:64, c * 128:(c + 1) * 128], identb[0:64, 0:64])
                    pt_ = PT[idx]
                    P.copy("act", pt_[:, :, 0:64], ptv)
                    pov = ps()
                    for c in range(6):
                        if c < 4:
                            vb = Vtok[:, rs // 2 + c, :] if rs % 2 == 0 else Vodd[:, (rs - 1) // 2 + c, :]
                        else:
                            vb = Vtok[:, 16 + (c - 4), :]
                        P.mm(pov.v(0, (ALL, slice(0, 64))), vb, pt_[:, c, 0:64], start=(c == 0), stop=(c == 5))
                    P.copy("dve", yT[po:po + 64, i * 64:(i + 1) * 64], pov.v(0, (slice(po, po + 64), slice(0, 64))))
                if with_ctx:
                    for qt in range(2):
                        idx = na_cnt[0] % 2
                        na_cnt[0] += 1
                        q0 = SEQ + qt * 128
                        pc = ps()
                        P.mm(pc.v(0, (ALL, slice(0, 256))), qT[po:po + 64, q0:q0 + 128], kT[po:po + 64, SEQ:T])
                        s_ = S_sb[idx]
                        P.copy("act", s_[:, 0:256], pc.v(0, (ALL, slice(0, 256))))
                        pn_ = softmax_rows(128, s_, 256, idx)
                        ptb = ps()
                        ptv = bank_bf16(ptb, 2, 128)
                        for c in range(2):
                            P.transpose(View(ptv.ap[:, c, :], ptv.regs), pn_[:, c * 128:(c + 1) * 128], identb[:, :])
                        pt_ = PT[idx]
                        P.copy("act", pt_[:, 0:2, :], ptv)
                        pov = ps()
                        for c in range(2):
                            P.mm(pov.v(0, (ALL, slice(0, 128))), Vtok[:, 16 + c, :], pt_[:, c, :], start=(c == 0), stop=(c == 1))
                        P.copy("dve", yT[po:po + 64, q0:q0 + 128], pov.v(0, (slice(po, po + 64), slice(0, 128))))
            if debug:
                Vflat = AVar(A, Vtok.off, (T,), BF16, Vtok.nwords)
                for di, src_ in enumerate((qT, kT, yT)):
                    for (c0, N) in TT:
                        P.copy("dve", outs[:, 0, 0:N], src_[:, c0:c0 + N])
                        P.dma(dbg[:, di, c0:c0 + N], outs[:, 0, 0:N])
                lidx = (na_cnt[0] - 1) % 2
                PTf = AVar(A, PT[lidx].off, (768,), BF16, PT[lidx].nwords)
                off_ = 0
                for src_, w_ in ((S_sb[lidx], 256), (Pe[lidx], 256), (Pn[lidx], 256), (PTf, 256), (identb, 128), (sm[lidx], 4)):
                    P.copy("dve", outs[:, 1, 0:w_], src_[:, 0:w_])
                    P.dma(dbg[:, 3, off_:off_ + w_], outs[:, 1, 0:w_])
                    off_ += w_
            wout_accumulate(l, b, 512 + hp * 128, 128, T if with_ctx else SEQ)


    A.top = mixbase
    TWO_PI = 6.283185307179586
    CW1 = 6.28125
    CW2 = float(np.float32(TWO_PI - CW1))
    CW3 = float(TWO_PI - CW1 - CW2)
    s5w = A.alloc((KC, 128), BF16)
    uT = A.alloc((4, T), BF16)
    ysum = A.alloc((T,), F32)
    s5tab = [A.alloc((2, 256), F32) for _ in range(2)]
    s5iota = A.alloc((256,), F32)
    s5tmp = [A.alloc((6, 256), F32) for _ in range(2)]
    s5ki = A.alloc((256,), I32)
    s5xs = [A.alloc((256,), BF16) for _ in range(2)]
    s5Bst = [A.alloc((2, 128), BF16) for _ in range(2)]
    s5Cst = A.alloc((8, 128), BF16)
    s5BD = A.alloc((2, 128), F32)
    s5p = A.alloc((16, 64), F32)
    s5c_r = A.alloc((128,), F32)
    s5sc = A.alloc((6, 2, 32), F32)
    s5carry = A.alloc((8,), F32)
    s5small = A.alloc((16,), F32)
    s5rowmask = A.alloc((8,), F32)
    s5sgn = A.alloc((1,), F32)
    s5dsk = A.alloc((4,), F32)
    s5glu = A.alloc((4, 512), BF16)
    s5sig = [AVar(A, s5tmp[i].off, (512,), F32, 512) for i in range(2)]
    s5_top = A.top
    print("s5 top", s5_top)
    s5cnt = [0]

    def range_wrap(out, in_, shift, scratch):
        if shift != 0.0:
            P.ts("dve", out, in_, shift, ALU.add)
        elif out is not in_:
            P.copy("dve", out, in_)
        P.ts("dve", scratch, out, -np.pi, ALU.is_lt, TWO_PI, ALU.mult)
        P.tt("dve", out, out, scratch, ALU.add)
        P.ts("dve", scratch, out, np.pi, ALU.is_gt, -TWO_PI, ALU.mult)
        P.tt("dve", out, out, scratch, ALU.add)

    def wrap_angle(dst, src, kf, ki):
        P.ts("dve", kf, src, 1.0 / TWO_PI, ALU.mult)
        P.copy("dve", ki, kf)
        P.copy("dve", kf, ki)
        P.stt(dst, kf, -CW1, src, ALU.mult, ALU.add)
        P.stt(dst, kf, -CW2, dst, ALU.mult, ALU.add)
        P.stt(dst, kf, -CW3, dst, ALU.mult, ALU.add)
        range_wrap(dst, dst, 0.0, kf)

    def sin_cos(dsin, dcos, ang, scratch):
        P.act(dsin, ang, AF.Sin)
        range_wrap(scratch, ang, np.pi / 2, dcos)
        P.act(dcos, scratch, AF.Sin)

    def s5_mixer(l, b):
        o = l // 2
        w_in = w_in_odd[o].rearrange("(k p) m -> p k m", p=128)
        with_ctx = l < DEPTH - 1
        tend = T if with_ctx else SEQ
        for cg in range(4):
            P.dma(s5w[:, :, :], w_in[:, :, 2064 + cg * 128:2064 + (cg + 1) * 128], queue="pool")
            for ti, (c0, N) in enumerate(TT):
                pu = ps()
                for k in range(KC):
                    P.mm(pu.v(0, (ALL, slice(0, N))), s5w[:, k, :], n[:, k, c0:c0 + N], start=(k == 0), stop=(k == KC - 1))
                P.copy("act" if ti % 2 == 0 else "dve", uT[:, cg, c0:c0 + N], pu.v(0, (ALL, slice(0, N))))
        P.dma(s5iota[:, :], s5_iota)
        P.dma(s5rowmask[:, :], s5_rowmask)
        P.dma(s5sgn[:, :], s5_sgn)
        P.dma(s5dsk[:, :], s5_dskip[o])
        P.dma(s5glu[:, :, :], d_glu[o].rearrange("(k p) m -> p k m", p=128), queue="pool")
        AR, AI, DT, RHO, OM, KF = (s5sc[:, i, :, :] for i in range(6))
        P.dma(AR, s5_are2[o])
        P.dma(AI, s5_aim2[o])
        P.dma(DT, s5_ldt2[o])
        P.act(DT, DT, AF.Exp)
        P.tt("dve", RHO, AR, DT, ALU.mult)
        P.act(RHO, RHO, AF.Exp)
        P.tt("dve", AI, AI, DT, ALU.mult)
        kiv = View(s5ki.ap[:, 0:64].rearrange("p (a b) -> p a b", b=32), s5ki[:, 0:64].regs)
        wrap_angle(OM, AI, KF, kiv)
        P.ts("dve", OM, OM, s5sgn[:, 0:1], ALU.mult)
        for cg in range(4):
            P.dma(s5c_r[:, :], s5_c_r[o][:, cg, :])
            P.ts("dve", s5c_r[:, :], s5c_r[:, :], s5sgn[:, 0:1], ALU.mult)
            P.memset("pool", s5Cst[:, :, :], 0.0)
            for gi in range(8):
                P.copy("pool", s5Cst[:, gi, gi * 16:(gi + 1) * 16], s5c_r[:, gi * 16:(gi + 1) * 16])
            for d in range(2):
                pr = lambda i: s5p[:, i, :]
                P.dma(pr(0), s5_are_r[o][:, cg, d, :])
                P.dma(pr(1), s5_aim_r[o][:, cg, d, :])
                P.dma(pr(2), s5_bre_r[o][:, cg, :])
                P.dma(pr(3), s5_bim_r[o][:, cg, :])
                P.dma(s5small[:, 0:1], s5_ldt_r[o][cg, d])
                P.act(s5small[:, 0:1], s5small[:, 0:1], AF.Exp)
                P.ts("dve", pr(4), pr(1), s5small[:, 0:1], ALU.mult)
                kiv2 = s5ki[:, 0:64]
                wrap_angle(pr(5), pr(4), pr(6), kiv2)
                sin_cos(pr(7), pr(8), pr(5), pr(6))
                P.act(pr(4), pr(0), AF.Exp, scale=s5small[:, 0:1])
                P.tt("dve", pr(8), pr(8), pr(4), ALU.mult)
                P.tt("dve", pr(7), pr(7), pr(4), ALU.mult)
                P.ts("dve", pr(8), pr(8), -1.0, ALU.add)
                P.tt("dve", pr(4), pr(0), pr(0), ALU.mult)
                P.tt("dve", pr(5), pr(1), pr(1), ALU.mult)
                P.tt("dve", pr(4), pr(4), pr(5), ALU.add)
                P.recip(pr(4), pr(4))
                P.tt("dve", pr(5), pr(8), pr(0), ALU.mult)
                P.tt("dve", pr(6), pr(7), pr(1), ALU.mult)
                P.tt("dve", pr(5), pr(5), pr(6), ALU.add)
                P.tt("dve", pr(5), pr(5), pr(4), ALU.mult)
                P.tt("dve", pr(6), pr(7), pr(0), ALU.mult)
                P.tt("dve", pr(9), pr(8), pr(1), ALU.mult)
                P.tt("dve", pr(6), pr(6), pr(9), ALU.subtract)
                P.tt("dve", pr(6), pr(6), pr(4), ALU.mult)
                P.tt("dve", pr(9), pr(5), pr(2), ALU.mult)
                P.tt("dve", pr(10), pr(6), pr(3), ALU.mult)
                P.tt("dve", s5BD[:, 0, 0:64], pr(9), pr(10), ALU.subtract)
                P.tt("dve", pr(9), pr(5), pr(3), ALU.mult)
                P.tt("dve", pr(10), pr(6), pr(2), ALU.mult)
                P.tt("dve", s5BD[:, 0, 64:128], pr(9), pr(10), ALU.add)
                P.copy("dve", s5BD[:, 1, 0:64], s5BD[:, 0, 64:128])
                P.copy("dve", s5BD[:, 1, 64:128], s5BD[:, 0, 0:64])
                accb = banks[3:8]
                if d == 0:
                    frames = [(SEQ, 256)] + [(c, 256) for c in range(0, SEQ, 256)]
                else:
                    frames = [(SEQ, 256)] + [(c, 256) for c in range(SEQ - 256, -1, -256)]
                for gi in range(8):
                    g = cg * 8 + gi
                    gidx = s5cnt[0] % 2
                    s5cnt[0] += 1
                    Bst = s5Bst[gidx]
                    P.ts("dve", Bst[:, 0, :], s5BD[:, 0, :], s5rowmask[:, gi:gi + 1], ALU.mult)
                    P.ts("dve", Bst[:, 1, :], s5BD[:, 1, :], s5rowmask[:, gi:gi + 1], ALU.mult)
                    tab = s5tab[gidx]
                    Ts_, Tc_ = tab[:, 0, :], tab[:, 1, :]
                    tmpA = s5tmp[0]
                    P.ts("dve", tmpA[:, 0, :], s5iota[:, :], s5sc[:, 4, d, g:g + 1], ALU.mult)
                    wrap_angle(tmpA[:, 1, :], tmpA[:, 0, :], tmpA[:, 2, :], s5ki[:, :])
                    sin_cos(Ts_, Tc_, tmpA[:, 1, :], tmpA[:, 2, :])
                    rv_ = s5sc[:, 3, d, g:g + 1]
                    rho_b = View(rv_.ap.to_broadcast([128, 256]), rv_.regs)
                    for fi, (c0, N) in enumerate(frames):
                        tmp_ = s5tmp[fi % 2]
                        xs_ = s5xs[fi % 2]
                        px = ps3()
                        P.mm(px.v(0, (ALL, slice(0, 256))), Bst[:, 0, :], uT[:, cg, c0:c0 + N])
                        P.mm(px.v(0, (ALL, slice(256, 512))), Bst[:, 1, :], uT[:, cg, c0:c0 + N])
                        if d == 0:
                            x1 = px.v(0, (ALL, slice(0, 256)))
                            x2 = px.v(0, (ALL, slice(256, 512)))
                        else:
                            x1 = View(px.t[:, 255::-1], px.regs)
                            x2 = View(px.t[:, 511:255:-1], px.regs)
                        P.tt("dve", tmp_[:, 0, :], x1, Tc_, ALU.mult)
                        P.tt("dve", tmp_[:, 1, :], x2, Ts_, ALU.mult)
                        P.tt("pool", tmp_[:, 2, :], tmp_[:, 0, :], tmp_[:, 1, :], ALU.add)
                        init = 0.0 if fi == 0 else s5carry[:, gi:gi + 1]
                        P.scan(tmp_[:, 3, :], rho_b, tmp_[:, 2, :], init, ALU.mult, ALU.add)
                        P.tt("pool", tmp_[:, 4, :], tmp_[:, 3, :], Tc_, ALU.mult)
                        P.copy("act", tmp_[0:64, 1, :], tmp_[64:128, 3, :])
                        P.copy("act", tmp_[64:128, 1, :], tmp_[0:64, 3, :])
                        P.tt("pool", tmp_[:, 5, :], tmp_[:, 1, :], Ts_, ALU.mult)
                        if d == 0:
                            xo = xs_[:, 0:256]
                        else:
                            xo = xs_.rev(ALL, 0, 256)
                        P.tt("pool", xo, tmp_[:, 4, :], tmp_[:, 5, :], ALU.subtract)
                        P.tt("dve", s5carry[:, gi:gi + 1], tmp_[:, 4, 255:256], tmp_[:, 5, 255:256], ALU.subtract)
                        bk = accb[c0 // 512]
                        P.mm(bk.v(0, (ALL, slice(c0 % 512, c0 % 512 + 256))), s5Cst[:, gi, :], xs_[:, 0:256],
                             start=(gi == 0), stop=(gi == 7))
                for (c0, N) in frames:
                    bk = accb[c0 // 512]
                    av = bk.v(0, (ALL, slice(c0 % 512, c0 % 512 + 256)))
                    if d == 0:
                        P.stt(ysum[:, c0:c0 + N], uT[:, cg, c0:c0 + N], s5dsk[:, cg:cg + 1], av, ALU.mult, ALU.add)
                    else:
                        P.tt("dve", ysum[:, c0:c0 + N], ysum[:, c0:c0 + N], av, ALU.add)
            for (c0, N) in TT:
                t0_, t1_ = s5sig[0], s5sig[1]
                P.act(t0_[:, 0:N], ysum[:, c0:c0 + N], AF.Square)
                P.ts("dve", t0_[:, 0:N], t0_[:, 0:N], 0.044715, ALU.mult, 1.0, ALU.add)
                P.tt("dve", t0_[:, 0:N], t0_[:, 0:N], ysum[:, c0:c0 + N], ALU.mult)
                P.act(t0_[:, 0:N], t0_[:, 0:N], AF.Sigmoid, scale=1.5957691216057308)
                P.tt("dve", uT[:, cg, c0:c0 + N], t0_[:, 0:N], ysum[:, c0:c0 + N], ALU.mult)
        for m in range(4):
            for ti, (c0, N) in enumerate(TT):
                if c0 >= tend:
                    continue
                pg = ps()
                for k in range(4):
                    P.mm(pg.v(0, (ALL, slice(0, N))), s5glu[:, k, m * 128:(m + 1) * 128], uT[:, k, c0:c0 + N], start=(k == 0), stop=(k == 3))
                sg_ = s5sig[ti % 2]
                P.act(sg_[:, 0:N], pg.v(0, (ALL, slice(0, N))), AF.Sigmoid)
                P.tt("dve", yT[:, c0:c0 + N], sg_[:, 0:N], uT[:, m, c0:c0 + N], ALU.mult)
            wout_accumulate(l, b, 512 + m * 128, 128, tend)


    A.top = mixbase
    CH = 64
    NCH = T // CH
    maskLT = A.alloc((64,), F32)
    maskUT = A.alloc((64,), F32)
    triMN = [A.alloc((128,), F32) for _ in range(2)]
    triP = [A.alloc((64,), F32) for _ in range(2)]

    def tri_inverse(Mv, Nv):
        Pc = triP[0]
        P.tt("dve", Pc[0:64, :], Mv, identf[0:64, 0:64], ALU.add)
        Mi, Ni = Mv, Nv
        for i in range(1, 6):
            psq = ps()
            P.mm(psq.v(0, (slice(0, 64), slice(0, 64))), Ni, Mi)
            P.mm(psq.v(0, (slice(0, 64), slice(64, 128))), Mi, Ni)
            mn = triMN[i % 2]
            P.copy("act", mn[0:64, :], psq.v(0, (slice(0, 64), slice(0, 128))))
            Mi, Ni = mn[0:64, 0:64], mn[0:64, 64:128]
            pp = ps()
            P.mm(pp.v(0, (slice(0, 64), slice(0, 64))), Ni, Pc[0:64, :])
            Pn_ = triP[i % 2]
            P.tt("dve", Pn_[0:64, :], Pc[0:64, :], pp.v(0, (slice(0, 64), slice(0, 64))), ALU.add)
            Pc = Pn_
        return Pc[0:64, :]

    chunk_base = A.top

    gw = [A.alloc((KC, 128), BF16) for _ in range(4)]
    gwg = A.alloc((KC, 16), BF16)
    g_cv = [A.alloc((256,), F32) for _ in range(2)]
    g_qn = A.alloc((T,), BF16)
    g_kn = A.alloc((T,), BF16)
    g_vT = A.alloc((T,), BF16)
    g_otok = A.alloc((NCH, 128), BF16)
    G4 = A.alloc((T,), F32)
    g16s = A.alloc((4,), F32)
    g_tokc = A.alloc((NCH, 16), F32)
    g_bc = A.alloc((3, 512), F32)
    g_conv_sb = A.alloc((12, 5), F32)
    g_onorm_sb = A.alloc((128,), F32)
    g_ch = [A.alloc((4, 64), BF16) for _ in range(2)]
    g_d2 = [A.alloc((64,), F32) for _ in range(2)]
    g_khat = [A.alloc((128,), BF16) for _ in range(2)]
    g_vtk = [A.alloc((128,), BF16) for _ in range(2)]
    g_E = [A.alloc((4, 64), F32) for _ in range(2)]
    g_MN = [A.alloc((128,), F32) for _ in range(2)]
    g_attn = [A.alloc((64,), BF16) for _ in range(2)]
    g_X = [A.alloc((128,), F32) for _ in range(2)]
    g_vn = [A.alloc((128,), BF16) for _ in range(2)]
    g_S = A.alloc((128,), F32)
    g_Sb = A.alloc((128,), BF16)
    g_ro = [A.alloc((2, 128), F32) for _ in range(2)]
    g_yb = [A.alloc((128,), BF16) for _ in range(2)]
    g_sm = [A.alloc((4,), F32) for _ in range(2)]
    gdn_top = A.top
    print("gdn top", gdn_top)
    gcnt = [0]

    def chunk_order(d):
        if d == 0:
            return list(range(32, 36)) + list(range(0, 32))
        return list(range(35, 31, -1)) + list(range(31, -1, -1))

    def onehot_lhsT(p):
        v = identf[:, p:p + 1]
        return View(v.ap.to_broadcast([128, 128]), v.regs)

    def gdn_mixer(l, b):
        o = l // 2
        w_in = w_in_odd[o].rearrange("(k p) m -> p k m", p=128)
        with_ctx = l < DEPTH - 1
        tend = T if with_ctx else SEQ
        P.dma(maskLT[0:64, :], g_maskLT)
        P.dma(maskUT[0:64, :], g_maskUT)
        P.dma(g_onorm_sb[0:64, :], g_onorm[o])
        P.dma(g_conv_sb[:, :, :], g_conv[o])
        P.dma(g16s[0:16, 0:1], g_alog[o])
        P.dma(g16s[0:16, 1:2], g_dtb[o])
        P.memset("dve", G4[:, :], 0.0)
        P.dma(gwg[:, :, :], w_in[:, :, 2048:2064], queue="pool")
        P.act(g16s[0:16, 0:1], g16s[0:16, 0:1], AF.Exp)
        P.ts("dve", g16s[0:16, 0:1], g16s[0:16, 0:1], -1.0, ALU.mult)
        for (c0, N) in TT:
            pg = ps()
            for k in range(KC):
                P.mm(pg.v(0, (slice(0, 16), slice(0, N))), gwg[:, k, :], n[:, k, c0:c0 + N], start=(k == 0), stop=(k == KC - 1))
            pgv = pg.v(0, (slice(0, 16), slice(0, N)))
            t0_, t1_ = tmp[0:16, 0, 0:N], tmp[0:16, 1, 0:N]
            P.act(t1_, pgv, AF.Sigmoid)
            P.act(t0_, pgv, AF.Exp, bias=g16s[0:16, 1:2])
            P.act(t0_, t0_, AF.Ln, bias=onescol[0:16, 0:1])
            P.ts("dve", t0_, t0_, g16s[0:16, 0:1], ALU.mult)
            P.copy("dve", G4[0:16, c0:c0 + N], t1_)
            P.copy("act", G4[32:48, c0:c0 + N], t0_)
        ones_b = View(onescol[32:48, 0:1].ap.to_broadcast([16, CH]), onescol[32:48, 0:1].regs)
        for c in range(NCH):
            P.scan(G4[64:80, c * CH:(c + 1) * CH], ones_b, G4[32:48, c * CH:(c + 1) * CH], 0.0, ALU.mult, ALU.add)
            P.scan(G4.rev(slice(96, 112), c * CH, CH), ones_b, G4.rev(slice(32, 48), c * CH, CH), 0.0, ALU.mult, ALU.add)
        for c in range(NCH):
            pt = ps()
            P.transpose(pt.v(0, (slice(0, 64), slice(0, 128))), G4[:, c * CH:(c + 1) * CH], identf[:, :])
            P.copy("act", g_tokc[0:64, c, 0:8], pt.v(0, (slice(0, 64), slice(0, 8))))
            P.copy("dve", g_tokc[0:64, c, 8:12], pt.v(0, (slice(0, 64), slice(64 + 8, 64 + 12))))
            P.copy("dve", g_tokc[0:64, c, 12:16], pt.v(0, (slice(0, 64), slice(96 + 12, 96 + 16))))
        for h in range(4):
            for wi, c_off in enumerate((0, 512, 1024, 1536)):
                P.dma(gw[wi][:, :, :], w_in[:, :, c_off + h * 128:c_off + (h + 1) * 128], queue="pool")
            cvi = 0
            for wi in range(3):
                dst = (g_qn, g_kn, g_vT)[wi]
                cw = lambda tau: g_conv_sb[:, wi * 4 + h, tau:tau + 1]
                for (s0, s1) in ((0, SEQ), (SEQ, T)):
                    for a0 in range(s0, s1, 256):
                        a1 = a0 + 256
                        lo, hi = max(s0, a0 - 2), min(s1, a1 + 2)
                        pp_ = ps()
                        for k in range(KC):
                            P.mm(pp_.v(0, (ALL, slice(0, hi - lo))), gw[wi][:, k, :], n[:, k, lo:hi], start=(k == 0), stop=(k == KC - 1))
                        cv = g_cv[cvi % 2]
                        cvi += 1
                        P.ts("dve", cv[:, 0:256], pp_.v(0, (ALL, slice(a0 - lo, a0 - lo + 256))), cw(2), ALU.mult)
                        for tau in (0, 1, 3, 4):
                            sh = tau - 2
                            o0, o1 = max(a0, lo - sh), min(a1, hi - sh)
                            P.stt(cv[:, o0 - a0:o1 - a0], pp_.v(0, (ALL, slice(o0 + sh - lo, o1 + sh - lo))), cw(tau),
                                  cv[:, o0 - a0:o1 - a0], ALU.mult, ALU.add)
                        P.act(cv[:, 0:256], cv[:, 0:256], AF.Silu)
                        if wi == 2:
                            P.copy("dve", dst[:, a0:a1], cv[:, 0:256])
                        else:
                            P.act(sq[:, 0, 0:256], cv[:, 0:256], AF.Square)
                            pb_ = ps()
                            P.mm(pb_.v(0, (ALL, slice(0, 256))), ones_bf[:, :], sq[:, 0, 0:256])
                            P.act(rstd[:, 0:256], pb_.v(0, (ALL, slice(0, 256))), AF.Sqrt, bias=eps12[:, 0:1], scale=(128.0 if wi == 0 else 1.0))
                            P.recip(rstd[:, 0:256], rstd[:, 0:256])
                            P.tt("dve", dst[:, a0:a1], cv[:, 0:256], rstd[:, 0:256], ALU.mult)
            for d in range(2):
                brow = d * 4 + h
                gslot = 64 + 8 + brow if d == 0 else 96 + 8 + brow
                gtc = 8 + h if d == 0 else 12 + h
                mM = maskUT if d == 0 else maskLT
                mN = maskLT if d == 0 else maskUT
                lastpos = CH - 1 if d == 0 else 0
                P.memset("dve", g_S[:, :], 0.0)
                P.memset("dve", g_Sb[:, :], 0.0)
                cur_tile = None
                for c in chunk_order(d):
                    tile_i = c // 8
                    lc = (c % 8) * CH
                    gc0 = c * CH
                    if tile_i != cur_tile:
                        cur_tile = tile_i
                        t0 = tile_i * 512
                        N = min(512, T - t0)
                        for bi, prow in enumerate((brow, gslot)):
                            pbc = ps()
                            P.mm(pbc.v(0, (ALL, slice(0, N))), onehot_lhsT(prow), G4[:, t0:t0 + N])
                            P.copy("act", g_bc[:, bi, 0:N], pbc.v(0, (ALL, slice(0, N))))
                        P.act(g_bc[:, 2, 0:N], g_bc[:, 1, 0:N], AF.Exp)
                    ix = gcnt[0] % 2
                    gcnt[0] += 1
                    chv = g_ch[ix]
                    kn_c = g_kn[:, gc0:gc0 + CH]
                    qn_c = g_qn[:, gc0:gc0 + CH]
                    kb, ktil, qtil, khT = chv[:, 0, :], chv[:, 1, :], chv[:, 2, :], chv[:, 3, :]
                    P.tt("dve", kb, kn_c, g_bc[:, 0, lc:lc + CH], ALU.mult)
                    P.tt("pool", ktil, kb, g_bc[:, 2, lc:lc + CH], ALU.mult)
                    P.tt("pool", qtil, qn_c, g_bc[:, 2, lc:lc + CH], ALU.mult)
                    d2 = g_d2[ix]
                    P.act(d2[:, :], g_bc[:, 1, lc:lc + CH], AF.Exp, bias=g_bc[:, 1, lc + lastpos:lc + lastpos + 1], scale=-1.0)
                    P.tt("pool", khT, kn_c, d2[:, :], ALU.mult)
                    ptb = ps()
                    ptv = bank_bf16(ptb, 2, 128)
                    P.transpose(View(ptv.ap[0:64, 0, :], ptv.regs), khT, identb[:, :])
                    P.transpose(View(ptv.ap[0:64, 1, :], ptv.regs), g_vT[:, gc0:gc0 + CH], identb[:, :])
                    khat = g_khat[ix]
                    vtk = g_vtk[ix]
                    P.copy("act", khat[0:64, :], View(ptv.ap[0:64, 0, :], ptv.regs))
                    P.copy("act", vtk[0:64, :], View(ptv.ap[0:64, 1, :], ptv.regs))
                    pa = ps()
                    pav = lambda i: pa.v(0, (slice(0, 64), slice(i * 64, (i + 1) * 64)))
                    P.mm(pav(0), kn_c, kb)
                    P.mm(pav(1), kb, kn_c)
                    P.mm(pav(2), kn_c, qn_c)
                    Ev = g_E[ix]
                    gcol = g_tokc[0:64, c, gtc:gtc + 1]
                    growv = g_bc[0:64, 1, lc:lc + CH]
                    P.stt(Ev[0:64, 0, :], growv, gcol, mM[0:64, :], ALU.subtract, ALU.add)
                    P.act(Ev[0:64, 0, :], Ev[0:64, 0, :], AF.Exp)
                    P.stt(Ev[0:64, 1, :], growv, gcol, mN[0:64, :], ALU.subtract, ALU.subtract)
                    P.act(Ev[0:64, 1, :], Ev[0:64, 1, :], AF.Exp, scale=-1.0)
                    P.tt("dve", g_attn[ix][0:64, :], pav(2), Ev[0:64, 0, :], ALU.mult)
                    P.tt("dve", Ev[0:64, 2, :], Ev[0:64, 0, :], identf[0:64, 0:64], ALU.subtract)
                    P.tt("dve", Ev[0:64, 3, :], Ev[0:64, 1, :], identf[0:64, 0:64], ALU.subtract)
                    mn = g_MN[ix]
                    P.stt(mn[0:64, 0:64], pav(0), -1.0, Ev[0:64, 2, :], ALU.mult, ALU.mult)
                    P.stt(mn[0:64, 64:128], pav(1), -1.0, Ev[0:64, 3, :], ALU.mult, ALU.mult)
                    TT_ = tri_inverse(mn[0:64, 0:64], mn[0:64, 64:128])
                    px = ps()
                    P.mm(px.v(0, (slice(0, 64), slice(0, 128))), ktil, g_Sb[:, :])
                    Xv = g_X[ix]
                    bcol = g_tokc[0:64, c, brow:brow + 1]
                    P.stt(Xv[0:64, :], vtk[0:64, :], bcol, px.v(0, (slice(0, 64), slice(0, 128))), ALU.mult, ALU.subtract)
                    pvn = ps()
                    P.mm(pvn.v(0, (slice(0, 64), slice(0, 128))), TT_, Xv[0:64, :])
                    vn = g_vn[ix]
                    P.copy("act", vn[0:64, :], pvn.v(0, (slice(0, 64), slice(0, 128))))
                    po_ = ps()
                    P.mm(po_.v(0, (slice(0, 64), slice(0, 128))), qtil, g_Sb[:, :], start=True, stop=False)
                    P.mm(po_.v(0, (slice(0, 64), slice(0, 128))), g_attn[ix][0:64, :], vn[0:64, :], start=False, stop=True)
                    pS = ps()
                    P.mm(pS.v(0, (ALL, slice(0, 128))), khat[0:64, :], vn[0:64, :])
                    elast = g_bc[:, 2, lc + lastpos:lc + lastpos + 1]
                    P.stt(g_S[:, :], g_S[:, :], elast, pS.v(0, (ALL, slice(0, 128))), ALU.mult, ALU.add)
                    P.copy("act", g_Sb[:, :], g_S[:, :])
                    pov = po_.v(0, (slice(0, 64), slice(0, 128)))
                    if d == 0:
                        P.copy("dve", g_otok[0:64, c, :], pov)
                    elif gc0 < tend:
                        ro = g_ro[ix]
                        smv = g_sm[ix]
                        P.tt("dve", ro[0:64, 0, :], pov, g_otok[0:64, c, :], ALU.add)
                        P.act(ro[0:64, 1, :], ro[0:64, 0, :], AF.Square, accum_out=smv[0:64, 0:1])
                        P.act(smv[0:64, 1:2], smv[0:64, 0:1], AF.Sqrt, bias=epsb[0:64, 0:1], scale=1.0 / 128)
                        P.recip(smv[0:64, 1:2], smv[0:64, 1:2])
                        P.stt(ro[0:64, 0, :], ro[0:64, 0, :], smv[0:64, 1:2], g_onorm_sb[0:64, :], ALU.mult, ALU.mult)
                        pz = ps()
                        for k in range(KC):
                            P.mm(pz.v(0, (slice(0, 64), slice(0, 128))), n[:, k, gc0:gc0 + CH], gw[3][:, k, :], start=(k == 0), stop=(k == KC - 1))
                        P.act(ro[0:64, 1, :], pz.v(0, (slice(0, 64), slice(0, 128))), AF.Silu)
                        yb = g_yb[ix]
                        P.tt("dve", yb[0:64, :], ro[0:64, 0, :], ro[0:64, 1, :], ALU.mult)
                        pty = ps()
                        ptyv = bank_bf16(pty, 1, 64)
                        P.transpose(View(ptyv.ap[:, 0, :], ptyv.regs), yb[0:64, :], identb[0:64, 0:64])
                        P.copy("act", yT[:, gc0:gc0 + CH], View(ptyv.ap[:, 0, :], ptyv.regs))
            wout_accumulate(l, b, h * 128, 128, tend)


    A.top = chunk_base
    RT = 256
    rww = [A.alloc((KC, 128), BF16) for _ in range(3)]
    r_lora = A.alloc((2, 512), BF16)
    r_g2 = A.alloc((512,), BF16)
    r_Lwa = A.alloc((2, 128), BF16)
    r_src = A.alloc((T,), BF16)
    r_r = A.alloc((T,), BF16)
    r_k = A.alloc((T,), BF16)
    r_v = A.alloc((T,), BF16)
    r_kkn = A.alloc((T,), BF16)
    r_twxa = A.alloc((T,), BF16)
    r_sxg = A.alloc((T,), BF16)
    r_otok = A.alloc((NCH, 128), BF16)
    r_bon0 = A.alloc((NCH, 2), F32)
    r_t5 = A.alloc((5, RT), F32)
    _save_top = A.top
    A.top = r_src.off
    r_AR = [A.alloc((128,), BF16) for _ in range(2)]
    r_BK = [A.alloc((128,), BF16) for _ in range(2)]
    r_FM = [A.alloc((128,), BF16) for _ in range(2)]
    r_FMT = [A.alloc((128,), BF16) for _ in range(2)]
    r_e = [A.alloc((4, 64), F32) for _ in range(2)]
    r_vtk0 = [A.alloc((128,), BF16) for _ in range(2)]
    assert A.top <= r_src.off + r_src.nwords
    A.top = _save_top
    r_ARh = [A.alloc((128,), BF16) for _ in range(2)]
    r_AX = [A.alloc((64,), BF16) for _ in range(2)]
    r_MKx = A.alloc((64,), F32)
    r_Z = [A.alloc((64,), BF16)] * 2
    r_BKh = A.alloc((128,), BF16)
    r_UV = [A.alloc((64,), BF16) for _ in range(4)]
    r_AAb = [A.alloc((128,), BF16) for _ in range(2)]
    r_mn = [A.alloc((128,), F32) for _ in range(2)]
    r_X = [A.alloc((64,), F32) for _ in range(2)]
    r_H = A.alloc((64,), F32)
    r_Hb = A.alloc((64,), BF16)
    r_ro = [A.alloc((2, 128), F32) for _ in range(2)]
    r_yb = [A.alloc((128,), BF16) for _ in range(2)]
    r_st = [A.alloc((32,), F32) for _ in range(2)]
    r_mu_sb = A.alloc((14,), F32)
    r_m6 = A.alloc((8,), F32)
    r_slot_sb = A.alloc((4,), F32)
    r_par = A.alloc((7, 2, 4), F32)
    r_lng_sb = A.alloc((128,), F32)
    r_lnb_sb = A.alloc((128,), F32)
    r_hsel = A.alloc((2,), BF16)
    r_hself = A.alloc((2,), F32)
    r_MK_sb = A.alloc((2, 128), F32)
    r_MN_sb = A.alloc((2, 64), F32)
    r_bones = A.alloc((128,), BF16)
    rwkv_top = A.top
    print("rwkv top", rwkv_top)
    rcnt = [0]

    def shift_mix(dst, chunk_idx, cast_eng="act"):
        mu = r_mu_sb[:, chunk_idx:chunk_idx + 1]
        P.ts("dve", r_m6[:, 0:4], r_slot_sb[:, 0:4], mu, ALU.mult)
        P.tt("dve", r_m6[:, 4:5], r_m6[:, 0:1], r_m6[:, 2:3], ALU.add)
        P.tt("dve", r_m6[:, 5:6], r_m6[:, 1:2], r_m6[:, 3:4], ALU.add)
        P.ts("dve", r_m6[:, 6:7], mu, -1.0, ALU.mult, 1.0, ALU.add)
        P.ts("dve", dst[:, 0:T], r_src[:, 0:T], r_m6[:, 6:7], ALU.mult)
        dl = dst[:, 0:SEQ]
        sl = r_src[:, 0:SEQ]
        d3 = dst.ap[:, 0:SEQ].rearrange("p (r c) -> p r c", c=64)
        s3 = r_src.ap[:, 0:SEQ].rearrange("p (r c) -> p r c", c=64)
        P.stt(View(d3[:, :, 1:64], dl.regs), View(s3[:, :, 0:63], sl.regs), r_m6[:, 0:1], View(d3[:, :, 1:64], dl.regs), ALU.mult, ALU.add)
        P.stt(View(d3[:, :, 0:63], dl.regs), View(s3[:, :, 1:64], sl.regs), r_m6[:, 1:2], View(d3[:, :, 0:63], dl.regs), ALU.mult, ALU.add)
        P.stt(dst[:, 64:SEQ], r_src[:, 0:SEQ - 64], r_m6[:, 2:3], dst[:, 64:SEQ], ALU.mult, ALU.add)
        P.stt(dst[:, 0:SEQ - 64], r_src[:, 64:SEQ], r_m6[:, 3:4], dst[:, 0:SEQ - 64], ALU.mult, ALU.add)
        P.stt(dst[:, SEQ + 1:T], r_src[:, SEQ:T - 1], r_m6[:, 4:5], dst[:, SEQ + 1:T], ALU.mult, ALU.add)
        P.stt(dst[:, SEQ:T - 1], r_src[:, SEQ + 1:T], r_m6[:, 5:6], dst[:, SEQ:T - 1], ALU.mult, ALU.add)

    def project_src(wt, evac_alt=0):
        for ti, (c0, N) in enumerate(TT):
            pp_ = ps()
            for k in range(KC):
                P.mm(pp_.v(0, (ALL, slice(0, N))), wt[:, k, :], n[:, k, c0:c0 + N], start=(k == 0), stop=(k == KC - 1))
            P.copy("act" if (ti + evac_alt) % 2 == 0 else "dve", r_src[:, c0:c0 + N], pp_.v(0, (ALL, slice(0, N))))

    def rwkv_mixer(l, b):
        ps_nb[0] = 7
        rwkv_mixer_(l, b)
        ps_nb[0] = 8

    def rwkv_mixer_(l, b):
        e = l // 2
        w_in = w_in_even[e].rearrange("(k p) m -> p k m", p=128)
        with_ctx = l < DEPTH - 1
        tend = T if with_ctx else SEQ
        P.dma(r_mu_sb[:, :], r_mu[e])
        P.dma(r_slot_sb[:, :], r_slot)
        P.dma(r_par[:, 0, :, :], r_w0[e])
        P.dma(r_par[:, 1, :, :], r_a0[e])
        P.dma(r_par[:, 2, 0, :], r_kk[e])
        P.dma(r_par[:, 3, 0, :], r_ka[e])
        P.dma(r_par[:, 4, 0, :], r_rk[e])
        P.ts("dve", r_par[:, 5, 0, :], r_par[:, 3, 0, :], -1.0, ALU.mult, 1.0, ALU.add)
        P.dma(r_hself[:, :], r_headsel)
        P.copy("dve", r_hsel[:, :], r_hself[:, :])
        P.dma(r_MK_sb[:, :, :], r_MK.rearrange("d p c -> p d c"))
        P.dma(r_MN_sb[0:64, :, :], r_MN.rearrange("d p c -> p d c"))
        P.dma(r_lora[0:64, :, :], a_w2[e].rearrange("d j c -> j d c"), queue="pool")
        P.dma(r_lora[64:128, :, :], a_a2[e].rearrange("d j c -> j d c"), queue="pool")
        P.dma(r_g2[:, :], a_g2[e], queue="pool")
        P.memset("dve", r_bones[:, :], 0.0)
        P.memset("dve", r_bones[0:64, 0:64], 1.0)
        P.memset("dve", r_bones[64:128, 64:128], 1.0)
        P.dma(rww[0][:, :, :], w_in[:, :, 1536:1664], queue="pool")
        project_src(rww[0])
        shift_mix(r_twxa, 12)
        P.act(r_twxa[0:64, :], r_twxa[0:64, :], AF.Tanh)
        P.dma(rww[1][:, :, :], w_in[:, :, 1664:1792], queue="pool")
        project_src(rww[1], 1)
        shift_mix(r_sxg, 13)
        P.act(r_sxg[:, :], r_sxg[:, :], AF.Sigmoid)
        if DBG["stage"] <= 1:
            return
        for hp in range(4):
            for wi, c_off in enumerate((0, 512, 1024)):
                P.dma(rww[wi][:, :, :], w_in[:, :, c_off + hp * 128:c_off + (hp + 1) * 128], queue="pool")
            P.dma(r_lng_sb[0:64, :], r_lng[e][:, hp * 128:(hp + 1) * 128])
            P.dma(r_lnb_sb[0:64, :], r_lnb[e][:, hp * 128:(hp + 1) * 128])
            for wi, dst in enumerate((r_r, r_k, r_v)):
                project_src(rww[wi], wi)
                shift_mix(dst, wi * 4 + hp)
            for (c0, N) in TT:
                P.ts("dve", tmp[:, 0, 0:N], r_k[:, c0:c0 + N], r_par[:, 2, 0, hp:hp + 1], ALU.mult)
                P.act(sq[:, 0, 0:N], tmp[:, 0, 0:N], AF.Square)
                pb_ = ps()
                P.mm(pb_.v(0, (ALL, slice(0, N))), r_bones[:, :], sq[:, 0, 0:N])
                P.act(rstd[:, 0:N], pb_.v(0, (ALL, slice(0, N))), AF.Sqrt, bias=eps12[:, 0:1])
                P.recip(rstd[:, 0:N], rstd[:, 0:N])
                P.tt("dve", r_kkn[:, c0:c0 + N], tmp[:, 0, 0:N], rstd[:, 0:N], ALU.mult)
            if debug and hp == 0 and DBG.get("dump") == "rkv":
                for di, s_ in enumerate((r_r, r_k, r_v, r_kkn)):
                    for (c0, N) in TT:
                        P.copy("dve", outs[:, 0, 0:N], s_[:, c0:c0 + N])
                        P.dma(dbg[:, di, c0:c0 + N], outs[:, 0, 0:N])
            if DBG["stage"] <= 2:
                return
            for d in range(2):
                if DBG["stage"] <= 6 and d == 1:
                    return
                lastpos = CH - 1 if d == 0 else 0
                MKd = r_MK_sb[:, d, :]
                P.memset("dve", r_H[:, :], 0.0)
                P.memset("dve", r_Hb[:, :], 0.0)
                P.memset("dve", r_Lwa[:, :, :], 0.0)
                P.memset("dve", r_MKx[:, :], 0.0)
                P.copy("dve", r_MKx[64:128, :], r_MK_sb[64:128, d, 0:64])
                for uv_ in r_UV:
                    P.memset("dve", uv_[:, :], 0.0)
                P.copy("dve", r_Lwa[0:64, 0, :], r_lora[0:64, d, hp * 128:(hp + 1) * 128])
                P.copy("dve", r_Lwa[64:128, 1, :], r_lora[64:128, d, hp * 128:(hp + 1) * 128])
                cur_tile = None
                for c in chunk_order(d):
                    tile_i = c // 4
                    lc = (c % 4) * CH
                    gc0 = c * CH
                    LW, AA_, CC, EE, E1 = (r_t5[:, i, :] for i in range(5))
                    if tile_i != cur_tile:
                        cur_tile = tile_i
                        t0 = tile_i * RT
                        pw = ps()
                        P.mm(pw.v(0, (ALL, slice(0, RT))), r_Lwa[:, 0, :], r_twxa[:, t0:t0 + RT])
                        P.mm(pw.v(0, (ALL, slice(RT, 2 * RT))), r_Lwa[:, 1, :], r_twxa[:, t0:t0 + RT])
                        if DBG["stage"] <= 2.05:
                            return
                        P.act(LW, pw.v(0, (ALL, slice(0, RT))), AF.Sigmoid, bias=r_par[:, 0, d, hp:hp + 1])
                        if DBG["stage"] <= 2.1:
                            return
                        P.ts("dve", LW, LW, -0.6065306597126334, ALU.mult)
                        P.act(AA_, pw.v(0, (ALL, slice(RT, 2 * RT))), AF.Sigmoid, bias=r_par[:, 1, d, hp:hp + 1])
                        if DBG["stage"] <= 2.2:
                            return
                        ones128 = View(onescol[:, 0:1].ap.to_broadcast([128, CH]), onescol[:, 0:1].regs)
                        for cc in range(RT // CH):
                            if d == 0:
                                P.scan(r_t5[:, 2, cc * CH:(cc + 1) * CH], ones128, r_t5[:, 0, cc * CH:(cc + 1) * CH], 0.0, ALU.mult, ALU.add)
                            else:
                                P.scan(r_t5.rev3(2, cc * CH, CH), ones128, r_t5.rev3(0, cc * CH, CH), 0.0, ALU.mult, ALU.add)
                        P.tt("dve", E1, CC, LW, ALU.subtract)
                        P.act(E1, E1, AF.Exp)
                        P.act(EE, CC, AF.Exp)
                        P.act(LW, CC, AF.Exp, scale=-1.0)
                    EINV = LW
                    if debug and DBG.get("dump") == "t5" and DBG["stage"] <= 2.5:
                        for i in range(5):
                            P.dma(dbg[:, 0, i * 256:(i + 1) * 256], r_t5[:, i, :])
                    if DBG["stage"] <= 2.5:
                        return
                    ix = rcnt[0] % 2
                    rcnt[0] += 1
                    AR, BK, FM = r_AR[ix], r_BK[ix], r_FM[ix]
                    ev = r_e[ix]
                    sl_ = slice(lc, lc + CH)
                    gs_ = slice(gc0, gc0 + CH)
                    a_c = r_t5[:, 1, sl_]
                    P.act(ev[:, 0, :], r_t5[:, 2, sl_], AF.Exp, bias=r_t5[:, 2, lc + lastpos:lc + lastpos + 1], scale=-1.0)
                    P.tt("dve", AR[:, 64:128], r_r[:, gs_], r_t5[:, 3, sl_], ALU.mult)
                    P.stt(AR[:, 0:64], r_kkn[:, gs_], -1.0, r_t5[:, 4, sl_], ALU.mult, ALU.mult)
                    P.tt("pool", ev[:, 3, :], r_kkn[:, gs_], a_c, ALU.mult)
                    P.tt("pool", BK[:, 0:64], ev[:, 3, :], r_t5[:, 0, sl_], ALU.mult)
                    P.tt("pool", FM[:, 0:64], ev[:, 3, :], ev[:, 0, :], ALU.mult)
                    P.ts("dve", ev[:, 1, :], a_c, r_par[:, 3, 0, hp:hp + 1], ALU.mult, r_par[:, 5, 0, hp:hp + 1], ALU.add)
                    P.tt("dve", ev[:, 2, :], r_k[:, gs_], ev[:, 1, :], ALU.mult)
                    P.tt("pool", BK[:, 64:128], ev[:, 2, :], r_t5[:, 0, sl_], ALU.mult)
                    P.tt("pool", FM[:, 64:128], ev[:, 2, :], ev[:, 0, :], ALU.mult)
                    if DBG["stage"] <= 2.7:
                        return
                    Zv = r_Z[ix]
                    P.stt(Zv[:, :], ev[:, 2, :], r_par[:, 4, 0, hp:hp + 1], r_r[:, gs_], ALU.mult, ALU.mult)
                    if DBG["stage"] <= 3:
                        return
                    ptb = ps()
                    ptv = bank_bf16(ptb, 2, 128)
                    P.transpose(View(ptv.ap[:, 0, :], ptv.regs), FM[:, :], identb[:, :])
                    P.transpose(View(ptv.ap[0:64, 1, :], ptv.regs), r_v[:, gs_], identb[:, :])
                    FMT = r_FMT[ix]
                    P.copy("act", FMT[:, :], View(ptv.ap[:, 0, :], ptv.regs))
                    vtk0 = r_vtk0[ix]
                    P.copy("act", vtk0[0:64, :], View(ptv.ap[0:64, 1, :], ptv.regs))
                    pbz = banks[7]
                    P.mm(pbz.v(0, (slice(0, 64), slice(128, 130))), Zv[:, :], r_hsel[:, :])
                    if DBG["stage"] <= 4:
                        return
                    pO = banks[7]
                    for hh in range(2):
                        po = hh * 64
                        jx = (rcnt[0] * 2 + hh) % 4
                        UV, AAb, mn, Xv = r_UV[jx], r_AAb[hh], r_mn[hh], r_X[hh]
                        P.copy("dve", UV[64:128, :], vtk0[0:64, po:po + 64])
                        if DBG["stage"] <= 4.05:
                            return
                        pA = ps()
                        ARh, AX = r_ARh[hh], r_AX[hh]
                        P.ts("dve", r_BKh[:, :], BK[:, :], r_hself[:, hh:hh + 1], ALU.mult)
                        P.ts("dve", ARh[:, :], AR[:, :], r_hself[:, hh:hh + 1], ALU.mult)
                        P.mm(pA.v(0, (ALL, slice(0, 128))), r_BKh[:, :], AR[:, :])
                        P.mm(pA.v(0, (slice(0, 64), slice(128, 192))), ARh[:, 0:64], BK[:, 0:64])
                        if DBG["stage"] <= 4.1:
                            return
                        P.tt("dve", AAb[:, :], pA.v(0, (ALL, slice(0, 128))), MKd, ALU.mult)
                        P.tt("dve", AX[:, :], pA.v(0, (ALL, slice(0, 64))), r_MKx[:, :], ALU.mult)
                        P.tt("dve", mn[0:64, 0:64], pA.v(0, (slice(0, 64), slice(0, 64))), r_MK_sb[0:64, d, 0:64], ALU.mult)
                        P.tt("dve", mn[0:64, 64:128], pA.v(0, (slice(0, 64), slice(128, 192))), r_MN_sb[0:64, d, :], ALU.mult)
                        if DBG["stage"] <= 4.2:
                            return
                        TT_ = tri_inverse(mn[0:64, 0:64], mn[0:64, 64:128])
                        if DBG["stage"] <= 4.3:
                            return
                        pX = ps()
                        P.mm(pX.v(0, (slice(0, 64), slice(0, 64))), ARh[:, 0:64], r_Hb[:, :], start=True, stop=False)
                        P.mm(pX.v(0, (slice(0, 64), slice(0, 64))), AX[:, :], UV[:, :], start=False, stop=True)
                        P.copy("act", Xv[0:64, :], pX.v(0, (slice(0, 64), slice(0, 64))))
                        if DBG["stage"] <= 4.4:
                            return
                        pU = ps()
                        P.mm(pU.v(0, (slice(0, 64), slice(0, 64))), TT_, Xv[0:64, :])
                        P.copy("act", UV[0:64, :], pU.v(0, (slice(0, 64), slice(0, 64))))
                        if DBG["stage"] <= 4.5:
                            return
                        pOv = pO.v(0, (slice(0, 64), slice(po, po + 64)))
                        P.mm(pOv, ARh[:, 64:128], r_Hb[:, :], start=True, stop=False)
                        P.mm(pOv, AAb[:, 64:128], UV[:, :], start=False, stop=True)
                        if DBG["stage"] <= 4.6:
                            return
                        pH = ps()
                        P.mm(pH.v(0, (ALL, slice(0, 64))), FMT[:, :], UV[:, :])
                        ecl = r_t5[po:po + 64, 3, lc + lastpos:lc + lastpos + 1]
                        P.stt(r_H[po:po + 64, :], r_H[po:po + 64, :], ecl, pH.v(0, (slice(po, po + 64), slice(0, 64))), ALU.mult, ALU.add)
                        P.copy("act", r_Hb[po:po + 64, :], r_H[po:po + 64, :])
                    if DBG["stage"] <= 5:
                        return
                    pOall = pO.v(0, (slice(0, 64), slice(0, 128)))
                    if d == 0:
                        P.copy("dve", r_otok[0:64, c, :], pOall)
                        P.copy("dve", r_bon0[0:64, c, :], pbz.v(0, (slice(0, 64), slice(128, 130))))
                    elif gc0 < tend:
                        ro = r_ro[ix]
                        st = r_st[ix]
                        P.tt("dve", ro[0:64, 0, :], pOall, r_otok[0:64, c, :], ALU.add)
                        P.tt("dve", st[0:64, 0:2], pbz.v(0, (slice(0, 64), slice(128, 130))), r_bon0[0:64, c, :], ALU.add)
                        for hh in range(2):
                            hs = slice(hh * 64, (hh + 1) * 64)
                            P.op("dve", (lambda e_, o_=st[0:64, 2 + hh * 6:8 + hh * 6].ap, i_=ro[0:64, 0, hs].ap: e_.bn_stats(o_, i_)),
                                 [ro[0:64, 0, hs]], [st[0:64, 2 + hh * 6:8 + hh * 6]])
                            P.op("dve", (lambda e_, o_=st[0:64, 14 + hh * 2:16 + hh * 2].ap, i_=st[0:64, 2 + hh * 6:8 + hh * 6].ap: e_.bn_aggr(o_, i_)),
                                 [st[0:64, 2 + hh * 6:8 + hh * 6]], [st[0:64, 14 + hh * 2:16 + hh * 2]])
                            P.act(st[0:64, 18 + hh:19 + hh], st[0:64, 15 + hh * 2:16 + hh * 2], AF.Sqrt, bias=epsgn[0:64, 0:1])
                            P.recip(st[0:64, 18 + hh:19 + hh], st[0:64, 18 + hh:19 + hh])
                            P.ts("dve", ro[0:64, 1, hs], ro[0:64, 0, hs], st[0:64, 14 + hh * 2:15 + hh * 2], ALU.subtract,
                                 st[0:64, 18 + hh:19 + hh], ALU.mult)
                            P.ts("dve", ro[0:64, 0, hs], vtk0[0:64, hs], st[0:64, hh:hh + 1], ALU.mult)
                        gsl = slice(hp * 128, (hp + 1) * 128)
                        P.tt("dve", ro[0:64, 1, :], ro[0:64, 1, :], r_lng_sb[0:64, :], ALU.mult)
                        P.tt("dve", ro[0:64, 1, :], ro[0:64, 1, :], r_lnb_sb[0:64, :], ALU.add)
                        P.tt("dve", ro[0:64, 1, :], ro[0:64, 1, :], ro[0:64, 0, :], ALU.add)
                        pg_ = ps()
                        P.mm(pg_.v(0, (slice(0, 64), slice(0, 128))), r_sxg[:, gs_], r_g2[:, gsl])
                        yb = r_yb[ix]
                        P.tt("dve", yb[0:64, :], ro[0:64, 1, :], pg_.v(0, (slice(0, 64), slice(0, 128))), ALU.mult)
                        pty = ps()
                        ptyv = bank_bf16(pty, 1, 64)
                        P.transpose(View(ptyv.ap[:, 0, :], ptyv.regs), yb[0:64, :], identb[0:64, 0:64])
                        P.copy("act", yT[:, gs_], View(ptyv.ap[:, 0, :], ptyv.regs))
            if debug:
                for (c0, N) in TT:
                    P.copy("dve", outs[:, 0, 0:N], yT[:, c0:c0 + N])
                    P.dma(dbg[:, hp, c0:c0 + N], outs[:, 0, 0:N])
            wout_accumulate(l, b, hp * 128, 128, tend)

    A.top = max(A.top, ffn_top, na_top, s5_top, gdn_top, rwkv_top)

    def rms_norm_tiles(c0, c1, scale_ap_fn, bias_ap_fn, dst, gain_only=False, out_dram=None):
        c = c0
        while c < c1:
            N = min(512, c1 - c)
            for k in range(KC):
                P.act(sq[:, k, 0:N], h[:, k, c:c + N], AF.Square)
            pb = ps()
            for k in range(KC):
                P.mm(pb.v(0, (ALL, slice(0, N))), ones_bf[:, :], sq[:, k, 0:N], start=(k == 0), stop=(k == KC - 1))
            P.act(rstd[:, 0:N], pb.v(0, (ALL, slice(0, N))), AF.Sqrt, bias=epsb[:, 0:1], scale=1.0 / D)
            P.recip(rstd[:, 0:N], rstd[:, 0:N])
            for k in range(KC):
                if gain_only:
                    t_ = dst[:, k, 0:N]
                    P.stt(t_, h[:, k, c:c + N], scale_ap_fn(k), rstd[:, 0:N], ALU.mult, ALU.mult)
                else:
                    t_ = tmp[:, k % 2, 0:N]
                    P.stt(t_, h[:, k, c:c + N], scale_ap_fn(k), rstd[:, 0:N], ALU.mult, ALU.mult)
                    P.act(dst[:, k, c:c + N], t_, AF.Identity, bias=bias_ap_fn(k))
            if out_dram is not None:
                P.dma(out_dram[:, :, c:c + N], dst[:, :, 0:N])
            c += N

    P.memset("dve", epsb[:, :], NORM_EPS)
    P.memset("dve", eps12[:, :], 1e-12)
    P.memset("dve", epsgn[:, :], 64e-5)
    P.memset("dve", onescol[:, :], 1.0)

    def make_ab(l, col, si, sh, gains):
        P.ts("dve", avec[:, :], modT[:, l, si * KC:(si + 1) * KC, col], 1.0, ALU.add)
        P.tt("dve", avec[:, :], avec[:, :], gains[:, l, :], ALU.mult)
        P.copy("dve", bvec[:, :], modT[:, l, sh * KC:(sh + 1) * KC, col])

    def streams(l):
        s = [(0, SEQ, None)]
        if l < DEPTH - 1:
            s.append((SEQ, T, NCOL - 1))
        return s

    for b in range(NB):
        P.dma(h[:, :, 0:SEQ], xT[b])
        P.dma(h[:, :, SEQ:T], ctxT[b])
        for l in layers:
            for (c0, c1, col) in [(0, SEQ, b), (SEQ, T, NCOL - 1)]:
                make_ab(l, col, 1, 0, nmix)
                rms_norm_tiles(c0, c1, lambda k: avec[:, k:k + 1], lambda k: bvec[:, k:k + 1], n)
            if mixers:
                if l % 2 == 0:
                    if mixers is True or "rwkv" in mixers:
                        rwkv_mixer(l, b)
                    if mixers is True or "na" in mixers:
                        na_mixer(l, b)
                else:
                    if mixers is True or "gdn" in mixers:
                        gdn_mixer(l, b)
                    if mixers is True or "s5" in mixers:
                        s5_mixer(l, b)
            for (c0, c1, col) in streams(l):
                col = b if col is None else col
                make_ab(l, col, 4, 3, nffn)
                rms_norm_tiles(c0, c1, lambda k: avec[:, k:k + 1], lambda k: bvec[:, k:k + 1], n)
            tend = T if l < DEPTH - 1 else SEQ
            wg_l = f_gate[l].rearrange("(k p) m -> p k m", p=128)
            wu_l = f_up[l].rearrange("(k p) m -> p k m", p=128)
            wd_l = f_down[l].rearrange("(j p) m -> p j m", p=128)
            for s0 in range(0, tend, 2 * FT):
                s1 = min(tend, s0 + 2 * FT)
                tiles = [(c, min(FT, s1 - c)) for c in range(s0, s1, FT)]
                for j in range(FJ):
                    w = wgu[wcnt[0] % 2]
                    wcnt[0] += 1
                    P.dma(w[:, 0, :, :], wg_l[:, :, j * 128:(j + 1) * 128], queue="pool")
                    P.dma(w[:, 1, :, :], wu_l[:, :, j * 128:(j + 1) * 128], queue="pool")
                    for ti, (c, N) in enumerate(tiles):
                        pg = ps()
                        pu = ps()
                        for k in range(KC):
                            P.mm(pg.v(0, (ALL, slice(0, N))), w[:, 0, k, :], n[:, k, c:c + N], start=(k == 0), stop=(k == KC - 1))
                        for k in range(KC):
                            P.mm(pu.v(0, (ALL, slice(0, N))), w[:, 1, k, :], n[:, k, c:c + N], start=(k == 0), stop=(k == KC - 1))
                        sgt = sg[ti % 2]
                        P.act(sgt[:, 0:N], pg.v(0, (ALL, slice(0, N))), AF.Silu)
                        P.tt("dve", hid[:, j, c - s0:c - s0 + N], sgt[:, 0:N], pu.v(0, (ALL, slice(0, N))), ALU.mult)
                for m in range(KC):
                    w = wdn[wcnt[1] % 2]
                    wcnt[1] += 1
                    P.dma(w[:, :, :], wd_l[:, :, m * 128:(m + 1) * 128], queue="pool")
                    for (c, N) in tiles:
                        po = ps()
                        for j in range(FJ):
                            P.mm(po.v(0, (ALL, slice(0, N))), w[:, j, :], hid[:, j, c - s0:c - s0 + N], start=(j == 0), stop=(j == FJ - 1))
                        for (a0, a1, col) in [(0, SEQ, b), (SEQ, T, NCOL - 1)]:
                            lo, hi = max(c, a0), min(c + N, a1)
                            if lo >= hi:
                                continue
                            P.stt(h[:, m, lo:hi], po.v(0, (ALL, slice(lo - c, hi - c))), modT[:, l, 5 * KC + m, col:col + 1],
                                  h[:, m, lo:hi], ALU.mult, ALU.add)
        osb = hid
        rms_norm_tiles(0, SEQ, lambda k: fnorm[:, k:k + 1], None, outs, gain_only=True, out_dram=outT[b])
    P.emit()
    return nc


def _fm(a):
    a = np.asarray(a, np.float32)
    k = a.shape[-1] // 128
    r = a.reshape(a.shape[:-1] + (k, 128))
    return np.ascontiguousarray(np.moveaxis(r, -1, 0))


def host_layout(inputs, b0, NB):
    x = np.asarray(inputs["x"], np.float32)[b0:b0 + NB]
    ctx = np.asarray(inputs["ctx"], np.float32)[b0:b0 + NB]
    c = np.asarray(inputs["c"], np.float32)[b0:b0 + NB]
    m = {}
    m["xT"] = np.ascontiguousarray(x.reshape(NB, SEQ, KC, 128).transpose(0, 3, 2, 1))
    m["ctxT"] = np.ascontiguousarray(ctx.reshape(NB, CTX, KC, 128).transpose(0, 3, 2, 1))
    cc = np.concatenate([c, np.asarray(inputs["c_ctx"], np.float32)[None]], 0)
    m["cT"] = np.ascontiguousarray(cc.reshape(NB + 1, KC, 128).transpose(2, 1, 0))
    m["mod_w"] = np.asarray(inputs["mod_w"], np.float32)
    m["mod_bT"] = _fm(inputs["mod_b"])
    m["nmixT"] = _fm(inputs["norm_mix"])
    m["nffnT"] = _fm(inputs["norm_ffn"])
    m["fnormT"] = _fm(inputs["final_norm"])
    for k in ("f_gate", "f_up", "f_down", "w_in_even", "w_in_odd", "w_out"):
        m[k] = np.asarray(inputs[k], np.float32)
    rpb = np.asarray(inputs["b_rpb"], np.float32)
    qq = np.arange(64)[:, None]
    kc = np.arange(64)[None, :]
    dc = np.clip(kc - qq + 15, 0, 30)
    g = rpb[:, :, :, dc]
    m["rpbT"] = np.ascontiguousarray(g.transpose(0, 3, 1, 2, 4).reshape(2, 64, 8, 960))
    ws = np.clip(qq - 8, 0, 48)
    ok = (kc >= ws) & (kc < ws + 16)
    m["maskB"] = np.ascontiguousarray(np.broadcast_to(np.where(ok, 0.0, -1e30).astype(np.float32)[:, None, :], (64, 15, 64)).reshape(64, 960))
    m["ident"] = np.eye(128, dtype=np.float32)
    are = np.asarray(inputs["d_are"], np.float32)
    aim = np.asarray(inputs["d_aim"], np.float32)
    ldt = np.asarray(inputs["d_logdt"], np.float32)
    bre = np.asarray(inputs["d_bre"], np.float32)
    bim = np.asarray(inputs["d_bim"], np.float32)
    cre = np.asarray(inputs["d_cre"], np.float32)
    cim = np.asarray(inputs["d_cim"], np.float32)
    t_ = are.transpose(0, 3, 1, 2)
    m["s5_are2"] = np.ascontiguousarray(np.concatenate([t_, t_], 1))
    t_ = aim.transpose(0, 3, 1, 2)
    m["s5_aim2"] = np.ascontiguousarray(np.concatenate([t_, t_], 1))
    m["s5_ldt2"] = np.ascontiguousarray(np.broadcast_to(ldt[:, None], (2, 128, 2, 32)))
    m["s5_sgn"] = np.concatenate([np.ones((64, 1), np.float32), -np.ones((64, 1), np.float32)], 0)
    a5 = are.reshape(2, 2, 4, 8, 64)
    m["s5_are_r"] = np.ascontiguousarray(np.repeat(a5.transpose(0, 3, 2, 1, 4), 16, axis=1))
    a5 = aim.reshape(2, 2, 4, 8, 64)
    m["s5_aim_r"] = np.ascontiguousarray(np.repeat(a5.transpose(0, 3, 2, 1, 4), 16, axis=1))
    l5 = ldt.reshape(2, 2, 4, 8)
    m["s5_ldt_r"] = np.ascontiguousarray(np.repeat(l5.transpose(0, 2, 1, 3), 16, axis=3))[..., None]
    b5 = bre.reshape(2, 4, 8, 64, 16)
    m["s5_bre_r"] = np.ascontiguousarray(b5.transpose(0, 2, 4, 1, 3).reshape(2, 128, 4, 64))
    b5 = bim.reshape(2, 4, 8, 64, 16)
    m["s5_bim_r"] = np.ascontiguousarray(b5.transpose(0, 2, 4, 1, 3).reshape(2, 128, 4, 64))
    c5 = cre.reshape(2, 4, 8, 16, 64).transpose(0, 4, 1, 2, 3).reshape(2, 64, 4, 128)
    c6 = cim.reshape(2, 4, 8, 16, 64).transpose(0, 4, 1, 2, 3).reshape(2, 64, 4, 128)
    m["s5_c_r"] = np.ascontiguousarray(np.concatenate([c5, c6], 1))
    m["s5_dskip"] = np.ascontiguousarray(np.asarray(inputs["d_d"], np.float32).reshape(2, 4, 128).transpose(0, 2, 1))
    m["s5_rowmask"] = (np.arange(128)[:, None] // 16 == np.arange(8)[None, :]).astype(np.float32)
    m["s5_iota"] = np.ascontiguousarray(np.broadcast_to(np.arange(1, 257, dtype=np.float32)[None], (128, 256)))
    m["d_glu"] = np.asarray(inputs["d_glu"], np.float32)
    def fm4(a):
        a = np.asarray(a, np.float32)
        return np.ascontiguousarray(np.moveaxis(a.reshape(a.shape[:-1] + (4, 128)), -1, 0))
    m["r_mu"] = np.ascontiguousarray(np.asarray(inputs["a_mu"], np.float32).reshape(2, 14, 128).transpose(0, 2, 1))
    m["r_slot"] = (np.arange(128)[:, None] % 4 == np.arange(4)[None, :]).astype(np.float32)
    m["r_w0"] = np.ascontiguousarray(fm4(inputs["a_w0"]).transpose(1, 0, 2, 3))
    m["r_a0"] = np.ascontiguousarray(fm4(inputs["a_a0"]).transpose(1, 0, 2, 3))
    m["r_kk"] = np.ascontiguousarray(fm4(inputs["a_kk"]).transpose(1, 0, 2))
    m["r_ka"] = np.ascontiguousarray(fm4(inputs["a_ka"]).transpose(1, 0, 2))
    m["r_rk"] = np.ascontiguousarray(fm4(np.asarray(inputs["a_rk"], np.float32).reshape(2, 512)).transpose(1, 0, 2))
    m["r_lng"] = np.ascontiguousarray(np.broadcast_to(np.asarray(inputs["a_lnx_g"], np.float32)[:, None, :], (2, 64, 512)))
    m["r_lnb"] = np.ascontiguousarray(np.broadcast_to(np.asarray(inputs["a_lnx_b"], np.float32)[:, None, :], (2, 64, 512)))
    for k in ("a_w2", "a_a2", "a_g2"):
        m[k] = np.asarray(inputs[k], np.float32)
    m["r_headsel"] = (np.arange(128)[:, None] // 64 == np.arange(2)[None, :]).astype(np.float32)
    r_ = np.arange(64)
    uts = (r_[:, None] < r_[None, :]).astype(np.float32)
    uti = (r_[:, None] <= r_[None, :]).astype(np.float32)
    mk0 = np.block([[uts, uti], [uts, uti]])
    mk1 = np.block([[uts.T, uti.T], [uts.T, uti.T]])
    m["r_MK"] = np.ascontiguousarray(np.stack([mk0, mk1]).astype(np.float32))
    m["r_MN"] = np.ascontiguousarray(np.stack([uts.T, uts]).astype(np.float32))
    r_ = np.arange(64)
    m["g_maskLT"] = np.where(r_[:, None] >= r_[None, :], 0.0, -30000.0).astype(np.float32)
    m["g_maskUT"] = np.where(r_[:, None] <= r_[None, :], 0.0, -30000.0).astype(np.float32)
    m["g_onorm"] = np.ascontiguousarray(np.broadcast_to(np.asarray(inputs["c_onorm"], np.float32)[:, None, :], (2, 64, 128)))
    cv = np.asarray(inputs["c_conv"], np.float32)
    m["g_conv"] = np.ascontiguousarray(cv.reshape(2, 5, 12, 128).transpose(0, 3, 2, 1))
    al = np.zeros((2, 16, 1), np.float32)
    db = np.zeros((2, 16, 1), np.float32)
    al[:, 8:16, 0] = np.asarray(inputs["c_alog"], np.float32).reshape(2, 8)
    db[:, 8:16, 0] = np.asarray(inputs["c_dtb"], np.float32).reshape(2, 8)
    m["g_alog"] = al
    m["g_dtb"] = db
    return m


_PROG_CACHE = {}


NB_PER_LAUNCH = 1


def kernel(**inputs):
    NCORES = 8
    NBC = 32 // NCORES
    NB = NB_PER_LAUNCH
    if NB not in _PROG_CACHE:
        _PROG_CACHE[NB] = build_program(NB)
    nc = _PROG_CACHE[NB]
    out = np.zeros((32, SEQ, D), np.float32)
    for li in range(NBC // NB):
        in_maps = [host_layout(inputs, c * NBC + li * NB, NB) for c in range(NCORES)]
        res = run_bass_kernel_spmd(nc, in_maps, core_ids=list(range(NCORES)))
        for c, r in enumerate(res.results):
            o = np.asarray(r["outT"])
            b0 = c * NBC + li * NB
            out[b0:b0 + NB] = o.transpose(0, 3, 2, 1).reshape(NB, SEQ, D)
    return out
```
